# Optimizing a Trainium2 kernel written in Bass

```python
import math
import jax, jax.numpy as jnp
from jax import lax
import numpy as np

D_MODEL = 1024
BATCH = 4
SEQ = 4096
DEPTH = 4

GRID_W = 64
CTX_LEN = 256
HEAD_DIM = 64
A_HEADS = 4
A_KV_HEADS = 2
B_HEADS = 4
B_QK_DIM = HEAD_DIM // 2
C_HEADS = 4
NA_KH = 8
NA_KW = 16
S5_GROUPS = 16
S5_GROUP_CH = 16
S5_STATE = 64
S5_DT_MIN = 1e-3
S5_DT_MAX = 1e-1

A_WIDTH = A_HEADS * HEAD_DIM
A_KV_WIDTH = A_KV_HEADS * HEAD_DIM
B_WIDTH = B_HEADS * HEAD_DIM
C_WIDTH = C_HEADS * HEAD_DIM
S5_WIDTH = S5_GROUPS * S5_GROUP_CH
MIX_WIDTH = A_WIDTH + B_WIDTH + C_WIDTH + S5_WIDTH
IN_SIZES = (A_WIDTH, A_KV_WIDTH, A_KV_WIDTH, A_WIDTH,
            B_WIDTH, B_WIDTH, B_WIDTH, B_WIDTH,
            C_WIDTH, C_WIDTH, C_WIDTH, C_WIDTH,
            S5_WIDTH, S5_WIDTH)
IN_WIDTH = sum(IN_SIZES)
IN_SPLITS = tuple(int(s) for s in np.cumsum(IN_SIZES)[:-1])

Q_BLOCK = 128
ROPE_BASE = 10000.0
RMS_EPS = 1e-6
LN_EPS = 1e-5
DEEPNORM_ALPHA = (2 * DEPTH) ** 0.25
DEEPNORM_BETA = (8 * DEPTH) ** -0.25

kernel_name = "hybrid_parallel_heads_dit_block"


def rms_norm(x, g):
    xf = x.astype(jnp.float32)
    y = xf * lax.rsqrt(jnp.mean(xf * xf, axis=-1, keepdims=True) + RMS_EPS)
    return (y * g).astype(x.dtype)


def layer_norm(x, g, b):
    xf = x.astype(jnp.float32)
    mu = jnp.mean(xf, axis=-1, keepdims=True)
    var = jnp.mean(jnp.square(xf - mu), axis=-1, keepdims=True)
    return ((xf - mu) * lax.rsqrt(var + LN_EPS) * g + b).astype(x.dtype)


def softmax32(s):
    return jax.nn.softmax(s.astype(jnp.float32), axis=-1)


def rope_1d(x, pos):
    half = x.shape[-1] // 2
    inv = ROPE_BASE ** (-jnp.arange(half, dtype=jnp.float32) / half)
    ang = pos.astype(jnp.float32)[:, None] * inv
    cos, sin = jnp.cos(ang).astype(x.dtype), jnp.sin(ang).astype(x.dtype)
    x1, x2 = x[..., :half], x[..., half:]
    return jnp.concatenate([x1 * cos - x2 * sin, x1 * sin + x2 * cos], axis=-1)


def rope_2d(x, row, col):
    h = x.shape[-1] // 2
    return jnp.concatenate([rope_1d(x[..., :h], row), rope_1d(x[..., h:], col)], axis=-1)


def split_heads(t, n):
    b, s, _ = t.shape
    return t.reshape(b, s, n, -1).transpose(0, 2, 1, 3)


def merge_heads(t):
    b, n, s, d = t.shape
    return t.transpose(0, 2, 1, 3).reshape(b, s, n * d)


def sweep_query_blocks(fn, qs):
    def to_blocks(q):
        qb = q.reshape(q.shape[:-2] + (q.shape[-2] // Q_BLOCK, Q_BLOCK, q.shape[-1]))
        return jnp.moveaxis(qb, -3, 0)
    out = lax.map(fn, tuple(to_blocks(q) for q in qs))
    out = jnp.moveaxis(out, 0, -3)
    return out.reshape(out.shape[:-3] + (-1, out.shape[-1]))


def attend(q, k, v, scale):
    p = softmax32(jnp.einsum('bkrqd,bktd->bkrqt', q, k) * scale).astype(v.dtype)
    return jnp.einsum('bkrqt,bktd->bkrqd', p, v)


def gqa_mixer(q_l, k_l, v_l, q_c, k_c, v_c, qn_g, kn_g, row, col, need_ctx):
    rep = A_HEADS // A_KV_HEADS
    scale = HEAD_DIM ** -0.5
    q = rope_2d(rms_norm(split_heads(q_l, A_HEADS), qn_g), row, col)
    k = rope_2d(rms_norm(split_heads(k_l, A_KV_HEADS), kn_g), row, col)
    v = split_heads(v_l, A_KV_HEADS)
    kc = rms_norm(split_heads(k_c, A_KV_HEADS), kn_g)
    vc = split_heads(v_c, A_KV_HEADS)
    k_all = jnp.concatenate([k, kc], axis=2)
    v_all = jnp.concatenate([v, vc], axis=2)

    def group(t):
        b, _, s, d = t.shape
        return t.reshape(b, A_KV_HEADS, rep, s, d)

    def ungroup(o):
        return o.reshape(o.shape[0], A_HEADS, o.shape[3], o.shape[4])

    o = sweep_query_blocks(lambda qs: attend(qs[0], k_all, v_all, scale), (group(q),))
    y_lat = merge_heads(ungroup(o))
    y_ctx = None
    if need_ctx:
        qc = group(rms_norm(split_heads(q_c, A_HEADS), qn_g))
        y_ctx = merge_heads(ungroup(attend(qc, kc, vc, scale)))
    return y_lat, y_ctx


def diff_attend(q1, q2, k1, k2, v, lam, scale):
    p1 = softmax32(jnp.einsum('bhqd,bhtd->bhqt', q1, k1) * scale)
    p2 = softmax32(jnp.einsum('bhqd,bhtd->bhqt', q2, k2) * scale)
    return jnp.einsum('bhqt,bhtd->bhqd', (p1 - lam * p2).astype(v.dtype), v)


def diff_mixer(q_l, k_l, v_l, q_c, k_c, v_c, lq1, lk1, lq2, lk2, subln_g, lam_init, row, col, need_ctx):
    scale = B_QK_DIM ** -0.5
    f32 = jnp.float32
    q = split_heads(q_l, B_HEADS)
    k = split_heads(k_l, B_HEADS)
    v = split_heads(v_l, B_HEADS)
    q1, q2 = rope_2d(q[..., :B_QK_DIM], row, col), rope_2d(q[..., B_QK_DIM:], row, col)
    k1, k2 = rope_2d(k[..., :B_QK_DIM], row, col), rope_2d(k[..., B_QK_DIM:], row, col)
    qc = split_heads(q_c, B_HEADS)
    kc = split_heads(k_c, B_HEADS)
    vc = split_heads(v_c, B_HEADS)
    kc1, kc2 = kc[..., :B_QK_DIM], kc[..., B_QK_DIM:]
    k1_all = jnp.concatenate([k1, kc1], axis=2)
    k2_all = jnp.concatenate([k2, kc2], axis=2)
    v_all = jnp.concatenate([v, vc], axis=2)
    lam = (jnp.exp(jnp.sum(lq1.astype(f32) * lk1.astype(f32)))
           - jnp.exp(jnp.sum(lq2.astype(f32) * lk2.astype(f32))) + lam_init)

    def post(o):
        return merge_heads(rms_norm(o, subln_g) * (1.0 - lam_init))

    o = sweep_query_blocks(
        lambda qs: diff_attend(qs[0], qs[1], k1_all, k2_all, v_all, lam, scale), (q1, q2))
    y_lat = post(o)
    y_ctx = None
    if need_ctx:
        y_ctx = post(diff_attend(qc[..., :B_QK_DIM], qc[..., B_QK_DIM:], kc1, kc2, vc, lam, scale))
    return y_lat, y_ctx


def na_mixer(q_l, k_l, v_l, q_c, k_c, v_c, rpb, need_ctx):
    bsz, seq, _ = q_l.shape
    rows = seq // GRID_W
    kh = min(NA_KH, rows)
    scale = HEAD_DIM ** -0.5
    q = split_heads(q_l, C_HEADS)
    k = split_heads(k_l, C_HEADS)
    v = split_heads(v_l, C_HEADS)
    kc = split_heads(k_c, C_HEADS)
    vc = split_heads(v_c, C_HEADS)

    def grid(t):
        return t.reshape(t.shape[0], t.shape[1], rows, GRID_W, t.shape[-1])

    qg, kg, vg = grid(q), grid(k), grid(v)
    wcol = jnp.arange(GRID_W)
    col_start = jnp.clip(wcol - NA_KW // 2, 0, GRID_W - NA_KW)
    col_idx = col_start[:, None] + jnp.arange(NA_KW)[None, :]
    dc_idx = col_idx - wcol[:, None] + (NA_KW - 1)

    def row_fn(args):
        r, q_row = args
        rs = jnp.clip(r - kh // 2, 0, rows - kh)
        k_win = lax.dynamic_slice_in_dim(kg, rs, kh, axis=2)[:, :, :, col_idx]
        v_win = lax.dynamic_slice_in_dim(vg, rs, kh, axis=2)[:, :, :, col_idx]
        dr_idx = rs + jnp.arange(kh) - r + (NA_KH - 1)
        bias = rpb[:, dr_idx[None, :, None], dc_idx[:, None, :]]
        s_win = jnp.einsum('bhwd,bhiwjd->bhwij', q_row, k_win) * scale + bias
        s_ctx = jnp.einsum('bhwd,bhtd->bhwt', q_row, kc) * scale
        b_, h_, w_ = s_win.shape[:3]
        p = softmax32(jnp.concatenate([s_win.reshape(b_, h_, w_, kh * NA_KW), s_ctx], axis=-1)).astype(v.dtype)
        p_win = p[..., :kh * NA_KW].reshape(b_, h_, w_, kh, NA_KW)
        p_ctx = p[..., kh * NA_KW:]
        return (jnp.einsum('bhwij,bhiwjd->bhwd', p_win, v_win)
                + jnp.einsum('bhwt,bhtd->bhwd', p_ctx, vc))

    out = lax.map(row_fn, (jnp.arange(rows), jnp.moveaxis(qg, 2, 0)))
    out = jnp.moveaxis(out, 0, 2).reshape(q.shape)
    y_lat = merge_heads(out)
    y_ctx = None
    if need_ctx:
        y_ctx = merge_heads(attend(split_heads(q_c, C_HEADS)[:, :, None], kc, vc, scale)[:, :, 0])
    return y_lat, y_ctx


def s5_discretise(a_re, a_im, log_dt, b_re, b_im):
    f32 = jnp.float32
    lam = lax.complex(a_re.astype(f32), a_im.astype(f32))
    dt = jnp.exp(log_dt.astype(f32))[:, None]
    lam_bar = jnp.exp(lam * dt)
    b = lax.complex(b_re.astype(f32), b_im.astype(f32))
    b_bar = ((lam_bar - 1.0) / lam)[..., None] * b
    return lam_bar, b_bar


def _ssm_combine(e1, e2):
    a1, b1 = e1
    a2, b2 = e2
    return a1 * a2, a2 * b1 + b2


def s5_scan(u, lam_bar, b_bar, h0, reverse):
    bu = jnp.einsum('gpc,blgc->blgp', b_bar, u.astype(jnp.complex64))
    if reverse:
        bu = jnp.flip(bu, axis=1)
    if h0 is not None:
        bu = bu.at[:, 0].add(lam_bar * h0)
    a = jnp.broadcast_to(lam_bar, bu.shape)
    _, h = lax.associative_scan(_ssm_combine, (a, bu), axis=1)
    h_last = h[:, -1]
    if reverse:
        h = jnp.flip(h, axis=1)
    return h, h_last


def s5_mixer(u_l, u_c, a_re, a_im, log_dt, b_re, b_im, c_re, c_im, d_skip, w_glu, need_ctx):
    f32 = jnp.float32

    def to_groups(u):
        return u.reshape(u.shape[0], u.shape[1], S5_GROUPS, S5_GROUP_CH)

    ug, ucg = to_groups(u_l), to_groups(u_c)
    disc_f = s5_discretise(a_re[0], a_im[0], log_dt[0], b_re[0], b_im[0])
    disc_b = s5_discretise(a_re[1], a_im[1], log_dt[1], b_re[1], b_im[1])
    c_f = lax.complex(c_re[0].astype(f32), c_im[0].astype(f32))
    c_b = lax.complex(c_re[1].astype(f32), c_im[1].astype(f32))

    def readout(h_f, h_b, u):
        y = (jnp.einsum('gcp,blgp->blgc', c_f, h_f).real
             + jnp.einsum('gcp,blgp->blgc', c_b, h_b).real)
        y = y.reshape(u.shape).astype(u.dtype) + d_skip * u
        z = jax.nn.gelu(y) @ w_glu
        z_val, z_gate = jnp.split(z, 2, axis=-1)
        return z_val * jax.nn.sigmoid(z_gate)

    hc_f, hf_last = s5_scan(ucg, disc_f[0], disc_f[1], None, False)
    hc_b, hb_last = s5_scan(ucg, disc_b[0], disc_b[1], None, True)
    hl_f, _ = s5_scan(ug, disc_f[0], disc_f[1], hf_last, False)
    hl_b, _ = s5_scan(ug, disc_b[0], disc_b[1], hb_last, True)
    y_lat = readout(hl_f, hl_b, u_l)
    y_ctx = readout(hc_f, hc_b, u_c) if need_ctx else None
    return y_lat, y_ctx


def hybrid_layer(x, ctx, c, c_ctx, row, col, layer_idx, need_ctx,
                 w_ada, b_ada, w_in, w_out, ln_g, ln_b, qn_g, kn_g,
                 lq1, lk1, lq2, lk2, subln_g, rpb,
                 a_re, a_im, log_dt, b_re, b_im, c_re, c_im, d_skip, w_glu):
    silu = jax.nn.silu
    shift, scale, gate = jnp.split(silu(c) @ w_ada + b_ada, 3, axis=-1)
    shift_c, scale_c, gate_c = jnp.split(silu(c_ctx) @ w_ada + b_ada, 3, axis=-1)
    h = x * (1.0 + scale[:, None]) + shift[:, None]
    hc = ctx * (1.0 + scale_c) + shift_c
    (aq, ak, av, ag, bq, bk, bv, bg, cq, ck, cv, cg, du, dg) = jnp.split(h @ w_in, IN_SPLITS, axis=-1)
    (aqc, akc, avc, agc, bqc, bkc, bvc, bgc, cqc, ckc, cvc, cgc, duc, dgc) = jnp.split(
        hc @ w_in, IN_SPLITS, axis=-1)
    lam_init = 0.8 - 0.6 * math.exp(-0.3 * layer_idx)

    ya, ya_c = gqa_mixer(aq, ak, av, aqc, akc, avc, qn_g, kn_g, row, col, need_ctx)
    yb, yb_c = diff_mixer(bq, bk, bv, bqc, bkc, bvc, lq1, lk1, lq2, lk2, subln_g, lam_init, row, col, need_ctx)
    yc, yc_c = na_mixer(cq, ck, cv, cqc, ckc, cvc, rpb, need_ctx)
    yd, yd_c = s5_mixer(du, duc, a_re, a_im, log_dt, b_re, b_im, c_re, c_im, d_skip, w_glu, need_ctx)

    y = jnp.concatenate([ya * silu(ag), yb * silu(bg), yc * silu(cg), yd * silu(dg)], axis=-1) @ w_out
    x_new = layer_norm(DEEPNORM_ALPHA * x + gate[:, None] * y, ln_g, ln_b)
    ctx_new = None
    if need_ctx:
        y_c = jnp.concatenate([ya_c * silu(agc), yb_c * silu(bgc), yc_c * silu(cgc), yd_c * silu(dgc)],
                              axis=-1) @ w_out
        ctx_new = layer_norm(DEEPNORM_ALPHA * ctx + gate_c * y_c, ln_g, ln_b)
    return x_new, ctx_new


def setup_inputs(seed: int = 0) -> dict:
    key = jax.random.key(seed)
    ks = jax.random.split(key, 27)
    f32 = jnp.float32

    def nrm(k, shape, s):
        return s * jax.random.normal(k, shape, f32)

    G, P, CH = S5_GROUPS, S5_STATE, S5_GROUP_CH
    return {
        "x": nrm(ks[0], (BATCH, SEQ, D_MODEL), 1.0),
        "c": nrm(ks[1], (BATCH, D_MODEL), 1.0),
        "ctx": nrm(ks[2], (BATCH, CTX_LEN, D_MODEL), 1.0),
        "c_ctx": nrm(ks[3], (D_MODEL,), 1.0),
        "w_ada": nrm(ks[4], (DEPTH, D_MODEL, 3 * D_MODEL), 0.5 * D_MODEL ** -0.5),
        "b_ada": nrm(ks[5], (DEPTH, 3 * D_MODEL), 0.01),
        "w_in": nrm(ks[6], (DEPTH, D_MODEL, IN_WIDTH), D_MODEL ** -0.5),
        "w_out": nrm(ks[7], (DEPTH, MIX_WIDTH, D_MODEL), DEEPNORM_BETA * MIX_WIDTH ** -0.5),
        "ln_g": 1.0 + nrm(ks[8], (DEPTH, D_MODEL), 0.01),
        "ln_b": nrm(ks[9], (DEPTH, D_MODEL), 0.01),
        "qn_g": 1.0 + nrm(ks[10], (DEPTH, HEAD_DIM), 0.01),
        "kn_g": 1.0 + nrm(ks[11], (DEPTH, HEAD_DIM), 0.01),
        "lam_q1": nrm(ks[12], (DEPTH, B_QK_DIM), 0.1),
        "lam_k1": nrm(ks[13], (DEPTH, B_QK_DIM), 0.1),
        "lam_q2": nrm(ks[14], (DEPTH, B_QK_DIM), 0.1),
        "lam_k2": nrm(ks[15], (DEPTH, B_QK_DIM), 0.1),
        "subln_g": 1.0 + nrm(ks[16], (DEPTH, HEAD_DIM), 0.01),
        "na_rpb": nrm(ks[17], (DEPTH, C_HEADS, 2 * NA_KH - 1, 2 * NA_KW - 1), 0.02),
        "s5_a_re": -0.5 + nrm(ks[18], (DEPTH, 2, G, P), 0.01),
        "s5_a_im": math.pi * jnp.arange(P, dtype=f32) + nrm(ks[19], (DEPTH, 2, G, P), 0.01),
        "s5_log_dt": jax.random.uniform(ks[20], (DEPTH, 2, G), f32,
                                        math.log(S5_DT_MIN), math.log(S5_DT_MAX)),
        "s5_b_re": nrm(ks[21], (DEPTH, 2, G, P, CH), (2 * CH) ** -0.5),
        "s5_b_im": nrm(ks[22], (DEPTH, 2, G, P, CH), (2 * CH) ** -0.5),
        "s5_c_re": nrm(ks[23], (DEPTH, 2, G, CH, P), (2 * P) ** -0.5),
        "s5_c_im": nrm(ks[24], (DEPTH, 2, G, CH, P), (2 * P) ** -0.5),
        "s5_d": nrm(ks[25], (DEPTH, S5_WIDTH), 0.5),
        "w_glu": nrm(ks[26], (DEPTH, S5_WIDTH, 2 * S5_WIDTH), S5_WIDTH ** -0.5),
    }


def reference(x, c, ctx, c_ctx, w_ada, b_ada, w_in, w_out, ln_g, ln_b, qn_g, kn_g,
              lam_q1, lam_k1, lam_q2, lam_k2, subln_g, na_rpb,
              s5_a_re, s5_a_im, s5_log_dt, s5_b_re, s5_b_im, s5_c_re, s5_c_im, s5_d, w_glu):
    seq = x.shape[1]
    t = jnp.arange(seq, dtype=jnp.int32)
    row, col = t // GRID_W, t % GRID_W
    for l in range(DEPTH):
        x, ctx = hybrid_layer(
            x, ctx, c, c_ctx, row, col, l, l < DEPTH - 1,
            w_ada[l], b_ada[l], w_in[l], w_out[l], ln_g[l], ln_b[l], qn_g[l], kn_g[l],
            lam_q1[l], lam_k1[l], lam_q2[l], lam_k2[l], subln_g[l], na_rpb[l],
            s5_a_re[l], s5_a_im[l], s5_log_dt[l], s5_b_re[l], s5_b_im[l], s5_c_re[l], s5_c_im[l],
            s5_d[l], w_glu[l])
    return x
```

```python
import math
import numpy as np
import concourse.bass as bass
import concourse.mybir as mybir
from concourse.bass_utils import run_bass_kernel_spmd
from contextlib import ExitStack

F32 = mybir.dt.float32
BF16 = mybir.dt.bfloat16
I32 = mybir.dt.int32
ALU = mybir.AluOpType
AF = mybir.ActivationFunctionType

COMPUTE = ("tensor", "scalar", "vector", "gpsimd")
QUEUES = ("sync",)
ALLENG = COMPUTE + QUEUES

D_MODEL = 1024
DEPTH = 4
NLAT = 4096
NCTX = 256
T = NLAT + NCTX
GRID_W = 64
ROPE_BASE = 10000.0
RMS_EPS = 1e-6
LN_EPS = 1e-5
ALPHA = (2 * DEPTH) ** 0.25
CHUNKS = [(i * 512, 512) for i in range(8)] + [(NLAT, NCTX)]
NKT = T // 128
NEXT = 2240
QA, KA, QB, KB, QC, KC, UU, GG = 0, 1, 2, 3, 4, 5, 6, 7
NFT = 11
NSW = 11
NPV = 47
NV = 21


def _c(*a, **k):
    return (a, k)


class Op:
    __slots__ = ("eng", "fn", "waits", "signal", "count", "dma_sem", "dma_count", "idx", "dma_inc")

    def __init__(self, eng, fn):
        self.eng = eng
        self.fn = fn
        self.waits = {}
        self.signal = False
        self.count = None
        self.dma_sem = None
        self.dma_count = None
        self.dma_inc = 16


class Prog:
    def __init__(self, nc):
        self.nc = nc
        self.ops = {e: [] for e in ALLENG}
        self.last_write = {}
        self.readers = {}
        self.dma_sems = {}
        self.pending = {e: {} for e in ALLENG}
        self.stack = ExitStack()

    def sbuf(self, name, shape, dtype):
        return self.stack.enter_context(self.nc.sbuf_tensor(name, list(shape), dtype))

    def psum(self, name, shape, dtype=F32):
        return self.stack.enter_context(self.nc.psum_tensor(name, list(shape), dtype))

    @staticmethod
    def _gt(a, b):
        ia = a if isinstance(a, int) else a.idx
        ib = b if isinstance(b, int) else b.idx
        return ia > ib

    def _addwait(self, op, d):
        if d is op:
            return
        if d.dma_sem is not None:
            key, val = ("dma", d.dma_sem), d.dma_count
        else:
            if d.eng == "tensor" and op.eng == "tensor":
                return
            key, val = d.eng, d
        cur = op.waits.get(key)
        if cur is None or self._gt(val, cur):
            op.waits[key] = val

    def _deps(self, op, reads, writes):
        for k, v in self.pending[op.eng].items():
            cur = op.waits.get(k)
            if cur is None or self._gt(v, cur):
                op.waits[k] = v
        self.pending[op.eng] = {}
        for r in reads:
            w = self.last_write.get(r)
            if w is not None:
                self._addwait(op, w)
        for w_ in writes:
            w = self.last_write.get(w_)
            if w is not None:
                self._addwait(op, w)
            for rd in self.readers.get(w_, ()):
                self._addwait(op, rd)
        for r in reads:
            self.readers.setdefault(r, []).append(op)
        for w_ in writes:
            self.last_write[w_] = op
            self.readers[w_] = []

    def op(self, eng, fn, reads=(), writes=()):
        o = Op(eng, fn)
        o.idx = len(self.ops[eng])
        self._deps(o, reads, writes)
        self.ops[eng].append(o)
        return o

    def dma(self, queue, sem, out, in_, reads=(), writes=()):
        o = Op(queue, ("dma_start", _c(out=out, in_=in_)))
        o.idx = len(self.ops[queue])
        self._deps(o, reads, writes)
        c = self.dma_sems.get(sem, 0) + 16
        self.dma_sems[sem] = c
        o.dma_sem = sem
        o.dma_count = c
        self.ops[queue].append(o)
        return o

    def collective(self, sem, kind, op, groups, in_ap, out_ap, reads=(), writes=()):
        o = Op("gpsimd", ("collective_compute", _c(kind, op, replica_groups=groups, ins=[in_ap], outs=[out_ap])))
        o.idx = len(self.ops["gpsimd"])
        self._deps(o, reads, writes)
        c = self.dma_sems.get(sem, 0) + 1
        self.dma_sems[sem] = c
        o.dma_sem = sem
        o.dma_count = c
        o.dma_inc = 1
        self.ops["gpsimd"].append(o)
        return o

    def barrier(self):
        for e in ALLENG:
            pend = self.pending[e]
            for e2 in COMPUTE:
                nd = [o_ for o_ in self.ops[e2] if o_.dma_sem is None]
                if nd and not (e == "tensor" and e2 == "tensor"):
                    last = nd[-1]
                    cur = pend.get(e2)
                    if cur is None or self._gt(last, cur):
                        pend[e2] = last
            for s, c in self.dma_sems.items():
                pend[("dma", s)] = c
        self.last_write = {}
        self.readers = {}

    def emit(self):
        nc = self.nc
        for e in self.ops:
            for o in self.ops[e]:
                for k, v in o.waits.items():
                    if not isinstance(v, int):
                        v.signal = True
        for e in COMPUTE:
            c = 0
            for o in self.ops[e]:
                if o.signal and o.dma_sem is None:
                    c += 1
                    o.count = c
        with ExitStack() as st:
            sems = {}
            for e in COMPUTE:
                sems[e] = st.enter_context(nc.semaphore("s_" + e))
            for name in self.dma_sems:
                sems[("dma", name)] = st.enter_context(nc.semaphore("d_" + name))
            block = st.enter_context(nc.Block())
            prog = self

            def run(engname):
                def body(eng):
                    waited = {}
                    for o in prog.ops[engname]:
                        for k, v in o.waits.items():
                            val = v if isinstance(v, int) else v.count
                            if waited.get(k, 0) >= val:
                                continue
                            eng.wait_ge(sems[k], val)
                            waited[k] = val
                        if callable(o.fn):
                            ins = o.fn(eng)
                        else:
                            meth, (a_, k_) = o.fn
                            ins = getattr(eng, meth)(*a_, **k_)
                        if o.dma_sem is not None:
                            if o.dma_inc == 16:
                                ins.then_inc(sems[("dma", o.dma_sem)], 16)
                            else:
                                ins.then_inc(sems[("dma", o.dma_sem)])
                        elif o.signal:
                            ins.then_inc(sems[engname], 1)
                    if engname == "sync":
                        for name, tot in prog.dma_sems.items():
                            eng.wait_ge(sems[("dma", name)], tot)
                        for e2 in COMPUTE:
                            cnts = [o.count for o in prog.ops[e2] if o.count]
                            if cnts:
                                eng.wait_ge(sems[e2], cnts[-1])
                return body

            block.tensor(run("tensor"))
            block.scalar(run("scalar"))
            block.vector(run("vector"))
            block.gpsimd(run("gpsimd"))
            block.sync(run("sync"))
        self.stack.close()


class Arena:
    def __init__(self, ap, size):
        self.ap = ap
        self.size = size
        self.off = 0

    def reset(self):
        self.off = 0

    def take(self, *shape):
        n = int(np.prod(shape))
        assert self.off + n <= self.size, (self.off, n, self.size)
        a = self.ap[:, self.off:self.off + n]
        self.off += n
        if len(shape) == 2:
            return a.rearrange("p (a b) -> p a b", a=shape[0])
        if len(shape) == 3:
            return a.rearrange("p (a b c) -> p a b c", a=shape[0], b=shape[1])
        return a


class RR:
    def __init__(self, items):
        self.items = items
        self.i = 0

    def next(self):
        it = self.items[self.i % len(self.items)]
        self.i += 1
        return it


def build_program(n_layers=DEPTH, debug=False):
    nc = bass.Bass("TRN2", target_bir_lowering=False)
    P = Prog(nc)

    def din(name, shape, dt=F32):
        return nc.dram_tensor(name, list(shape), dt, kind="ExternalInput").ap()

    x_tok = din("x_tok", [T, 1024])
    cvec = din("cvec", [128, 8, 2])
    w_ada = din("w_ada", [DEPTH, 1024, 3072])
    w_ext = din("w_ext", [DEPTH, 1024, NEXT])
    w_out = din("w_out", [DEPTH, 1024, 1024])
    w_glu = din("w_glu", [DEPTH, 256, 256])
    pvec = din("pvec", [DEPTH, 128, NPV])
    lamv = din("lamv", [DEPTH, 128, 128])
    rope = din("rope", [4, 128, T])
    nabias = din("nabias", [DEPTH, 2, 128, NV * 128])
    s5pl = din("s5pl", [DEPTH, 128, 48])
    s5row = din("s5row", [DEPTH, 3, 1024])
    s5b = din("s5b", [DEPTH, 2, 128, 1024])
    s5c = din("s5c", [DEPTH, 2, 128, 2048])
    consts = din("consts", [128, 1288])
    out = nc.dram_tensor("out", [NLAT, 1024], F32, kind="ExternalOutput").ap()
    skind = "ExternalOutput" if debug else "Internal"
    xT = nc.dram_tensor("xT", [1024, T], F32, kind=skind).ap()
    fT = nc.dram_tensor("fT", [NFT * 128, T], BF16, kind=skind).ap()
    vall = nc.dram_tensor("vall", [T, 320], BF16, kind=skind).ap()
    mixL = nc.dram_tensor("mixL", [512, T], BF16, kind="Internal").ap()
    mixT = nc.dram_tensor("mixT", [1024, T], BF16, kind="Internal").ap()
    geL = nc.dram_tensor("geL", [128, T], BF16, kind="Internal").ap()
    geF = nc.dram_tensor("geF", [256, T], BF16, kind="Internal").ap()
    PAIRS = [[0, 1], [2, 3], [4, 5], [6, 7]]

    cst = P.sbuf("cst", [128, 1288], F32)
    ident = cst[:, 0:128]
    shift64 = cst[:, 128:256]
    iota_f = cst[:, 256:768]
    iota_r = cst[:, 768:1280]
    sgn = cst[:, 1280:1281]
    ones_bf = P.sbuf("ones_bf", [128, 128], BF16)
    blk_bf = P.sbuf("blk_bf", [128, 128], BF16)
    sc = P.sbuf("sc", [128, 8, 2], F32)
    pv = P.sbuf("pv", [128, NPV], F32)
    modsb = P.sbuf("modsb", [128, 24, 2], F32)
    sc1 = P.sbuf("sc1", [128, 8, 2], F32)
    lamt = P.sbuf("lamt", [128, 128], F32)
    lams = P.sbuf("lams", [128, 8], F32)
    epsb = P.sbuf("epsb", [128, 1], F32)
    i32a = P.sbuf("i32a", [128, 32], I32)
    i32b = P.sbuf("i32b", [128, 512], I32)
    i32c = P.sbuf("i32c", [128, 512], I32)
    a32t = P.sbuf("a32", [128, 23552], F32)
    a16t = P.sbuf("a16", [128, 46592], BF16)
    A32 = Arena(a32t, 23552)
    A16 = Arena(a16t, 46592)
    pp = [P.psum(f"pp{i}", [128, 1024], F32) for i in range(4)]
    pb = [pp[i // 2][:, (i % 2) * 512:(i % 2 + 1) * 512] for i in range(8)]

    def PB(i):
        return ("pb", i)

    P.dma("sync", "cst", cst[:], consts, writes=["cst"])
    P.op("vector", ("memset", _c(ones_bf[:], 1.0)), writes=["ones_bf"])
    P.op("vector", ("memset", _c(epsb[:], RMS_EPS)), writes=["epsb"])
    P.op("vector", ("memset", _c(blk_bf[:], 0.0)), writes=["blk_bf"])
    P.op("vector", ("memset", _c(blk_bf[0:64, 0:64], 1.0)), reads=["blk_bf"], writes=["blk_bf"])
    P.op("vector", ("memset", _c(blk_bf[64:128, 64:128], 1.0)), reads=["blk_bf"], writes=["blk_bf"])
    P.dma("sync", "sc", sc[:], cvec, writes=["sc"])
    P.op("scalar", ("activation", _c(out=sc[:], in_=sc[:], func=AF.Silu)), reads=["sc"], writes=["sc"])

    xTv = xT.rearrange("(ft p) t -> p ft t", p=128)
    cp_rr = [0]

    def copy_rr(out_ap, in_ap, reads, writes):
        cp_rr[0] += 1
        if cp_rr[0] % 2:
            return P.op("vector", ("tensor_copy", _c(out=out_ap, in_=in_ap)), reads=reads, writes=writes)
        return P.op("scalar", ("copy", _c(out=out_ap, in_=in_ap)), reads=reads, writes=writes)

    def phase0():
        A32.reset()
        xin = [A32.take(1024) for _ in range(4)]
        stage = A32.take(8, 512)
        for ci, (c0, n) in enumerate(CHUNKS):
            ntt = n // 128
            for tt in range(ntt):
                P.dma("sync", f"xin{tt}", xin[tt], x_tok[c0 + tt * 128:c0 + (tt + 1) * 128, :], writes=[("xin", tt)])
            for ft in range(8):
                for tt in range(ntt):
                    P.op("tensor", ("transpose", _c(out=pb[ft][:, tt * 128:(tt + 1) * 128],
                                                                   in_=xin[tt][:, ft * 128:(ft + 1) * 128], identity=ident)),
                         reads=[("xin", tt), "cst"], writes=[PB(ft)])
                copy_rr(stage[:, ft, 0:n], pb[ft][:, 0:n], [PB(ft)], [("xst", ft)])
            P.dma("gpsimd", "xst", xTv[:, :, c0:c0 + n], stage[:, :, 0:n], reads=[("xst", f) for f in range(8)],
                  writes=[("xT", ci)])

    def phase_ada(l):
        A32.reset()
        wst = [A32.take(8, 512) for _ in range(2)]
        P.dma("sync", "pv", pv[:], pvec[l], writes=["pv"])
        P.dma("sync", "lamt", lamt[:], lamv[l], writes=["lamt"])
        wv = w_ada[l].rearrange("(kt p) n -> p kt n", p=128)
        for piece in range(6):
            w = wst[piece % 2]
            P.dma("sync", f"wst{piece % 2}", w, wv[:, :, piece * 512:(piece + 1) * 512], writes=[("wst", piece % 2)])
            for j in range(4):
                ft = piece * 4 + j
                for kt in range(8):
                    P.op("tensor", ("matmul", _c(
                        pb[0][:, ft * 2:ft * 2 + 2], lhsT=w[:, kt, j * 128:(j + 1) * 128], rhs=sc[:, kt, :],
                        start=(kt == 0), stop=(kt == 7))), reads=[("wst", piece % 2), "sc"], writes=[PB(0)])
        P.op("vector", ("tensor_tensor", _c(
            out=modsb[:], in0=pb[0][:, 0:48].rearrange("p (a b) -> p a b", b=2),
            in1=pv[:, 23:47].unsqueeze(2).to_broadcast([128, 24, 2]), op=ALU.add)),
            reads=[PB(0), "pv"], writes=["modsb"])
        P.op("vector", ("tensor_scalar", _c(out=sc1[:], in0=modsb[:, 8:16, :], scalar1=1.0, scalar2=None, op0=ALU.add)),
             reads=["modsb"], writes=["sc1"])
        lam_init = 0.8 - 0.6 * math.exp(-0.3 * l)
        P.op("vector", ("tensor_tensor", _c(out=lamt[:, 0:32], in0=lamt[:, 0:32], in1=lamt[:, 32:64], op=ALU.mult)),
             reads=["lamt"], writes=["lamt"])
        P.op("vector", ("tensor_tensor", _c(out=lamt[:, 64:96], in0=lamt[:, 64:96], in1=lamt[:, 96:128], op=ALU.mult)),
             reads=["lamt"], writes=["lamt"])
        P.op("vector", ("reduce_sum", _c(out=lams[:, 0:1], in_=lamt[:, 0:32], axis=mybir.AxisListType.X)),
             reads=["lamt"], writes=["lams"])
        P.op("vector", ("reduce_sum", _c(out=lams[:, 1:2], in_=lamt[:, 64:96], axis=mybir.AxisListType.X)),
             reads=["lamt", "lams"], writes=["lams"])
        P.op("scalar", ("activation", _c(out=lams[:, 2:4], in_=lams[:, 0:2], func=AF.Exp)), reads=["lams"], writes=["lams"])
        P.op("vector", ("tensor_tensor", _c(out=lams[:, 4:5], in0=lams[:, 3:4], in1=lams[:, 2:3], op=ALU.subtract)),
             reads=["lams"], writes=["lams"])
        P.op("vector", ("tensor_scalar", _c(out=lams[:, 5:6], in0=lams[:, 4:5], scalar1=-lam_init, scalar2=None, op0=ALU.add)),
             reads=["lams"], writes=["lams"])
        P.op("vector", ("tensor_scalar", _c(out=lams[:, 6:7], in0=pv[:, 4:5], scalar1=(1.0 - lam_init), scalar2=None, op0=ALU.mult)),
             reads=["lams", "pv"], writes=["lams"])

    def phase_in(l):
        A32.reset()
        A16.reset()
        wst = [A32.take(8, 140) for _ in range(2)]
        xk = [A32.take(512) for _ in range(3)]
        tabs = [A32.take(4, 512) for _ in range(2)]
        wk = [[A32.take(512) for _ in range(4)] for _ in range(2)]
        wbf = A16.take(8, NEXT)
        hT = [A16.take(8, 512) for _ in range(2)]
        ost = [A16.take(512) for _ in range(4)]
        sqb = [A16.take(512) for _ in range(2)]
        vst = [A16.take(320) for _ in range(2)]
        wv = w_ext[l].rearrange("(kt p) n -> p kt n", p=128)
        for piece in range(16):
            w = wst[piece % 2]
            cs = slice(piece * 140, (piece + 1) * 140)
            P.dma("sync", f"wst{piece % 2}", w, wv[:, :, cs], writes=[("wst", piece % 2)])
            copy_rr(wbf[:, :, cs], w, [("wst", piece % 2)], [("wbf", piece)])
        wres = [("wbf", p_) for p_ in range(16)]
        bank_rr = RR(list(range(8)))
        ost_rr = RR(list(range(4)))
        fTv = fT.rearrange("(m p) t -> p m t", p=128)
        vallv = vall.rearrange("(tt p) d -> p tt d", p=128)
        for ci, (c0, n) in enumerate(CHUNKS):
            j = 0 if c0 < NLAT else 1
            h = hT[ci % 2]
            hres = ("hT", ci % 2)
            for kt in range(8):
                xb = xk[kt % 3]
                P.dma("sync", f"xk{kt % 3}", xb[:, 0:n], xTv[:, kt, c0:c0 + n], reads=[("xT", ci)], writes=[("xk", kt % 3)])
                P.op("scalar", ("activation", _c(
                    out=h[:, kt, 0:n], in_=xb[:, 0:n], func=AF.Identity, bias=modsb[:, kt, j:j + 1], scale=sc1[:, kt, j:j + 1])),
                    reads=[("xk", kt % 3), "modsb", "sc1"], writes=[hres])
            tb = tabs[ci % 2]
            P.dma("sync", f"tabs{ci % 2}", tb[:, :, 0:n], rope.rearrange("k p t -> p k t")[:, :, c0:c0 + n], writes=[("tabs", ci % 2)])
            tres = ("tabs", ci % 2)

            def proj(colbase, bank):
                for kt in range(8):
                    P.op("tensor", ("matmul", _c(pb[bank][:, 0:n], lhsT=wbf[:, kt, colbase:colbase + 128],
                                                             rhs=h[:, kt, 0:n], start=(kt == 0), stop=(kt == 7))),
                         reads=wres + [hres], writes=[PB(bank)])

            def store(m, o, ores):
                P.dma("gpsimd", f"ost{ores[1]}", fTv[:, m, c0:c0 + n], o[:, 0:n], reads=[ores], writes=[("fT", m, ci)])

            for m in range(NFT):
                b1 = bank_rr.next()
                proj(m * 128, b1)
                oi = ost_rr.next()
                o = ost[oi]
                ores = ("ost", oi)
                if m in (QA, KA):
                    sw = NSW + m
                    gcol = 0 if m == QA else 2
                    b2 = bank_rr.next()
                    proj(sw * 128, b2)
                    b3 = bank_rr.next()
                    w_ = wk[m % 2]
                    wr = [("wk", m % 2, k) for k in range(4)]
                    sq = sqb[m % 2]
                    P.op("scalar", ("activation", _c(out=sq[:, 0:n], in_=pb[b1][:, 0:n], func=AF.Square)),
                         reads=[PB(b1)], writes=[("sqb", m % 2)])
                    P.op("tensor", ("matmul", _c(pb[b3][:, 0:n], lhsT=blk_bf[:], rhs=sq[:, 0:n], start=True, stop=True)),
                         reads=[("sqb", m % 2), "blk_bf"], writes=[PB(b3)])
                    P.op("scalar", ("activation", _c(
                        out=w_[0][:, 0:n], in_=pb[b1][:, 0:n], func=AF.Identity, scale=pv[:, gcol:gcol + 1])),
                        reads=[PB(b1), "pv"], writes=[wr[0]])
                    P.op("scalar", ("activation", _c(
                        out=w_[1][:, 0:n], in_=pb[b2][:, 0:n], func=AF.Identity, scale=pv[:, gcol + 1:gcol + 2])),
                        reads=[PB(b2), "pv"], writes=[wr[1]])
                    P.op("scalar", ("activation", _c(
                        out=w_[2][:, 0:n], in_=pb[b3][:, 0:n], func=AF.Ln, scale=1.0 / 64.0, bias=epsb[:, 0:1])),
                        reads=[PB(b3), "epsb"], writes=[wr[2]])
                    P.op("scalar", ("activation", _c(out=w_[2][:, 0:n], in_=w_[2][:, 0:n], func=AF.Exp, scale=-0.5)), reads=[wr[2]], writes=[wr[2]])
                    P.op("vector", ("tensor_tensor", _c(out=w_[0][:, 0:n], in0=w_[0][:, 0:n], in1=tb[:, 0, 0:n], op=ALU.mult)),
                         reads=[wr[0], tres], writes=[wr[0]])
                    P.op("vector", ("tensor_tensor", _c(out=w_[1][:, 0:n], in0=w_[1][:, 0:n], in1=tb[:, 1, 0:n], op=ALU.mult)),
                         reads=[wr[1], tres], writes=[wr[1]])
                    P.op("vector", ("tensor_tensor", _c(out=w_[0][:, 0:n], in0=w_[0][:, 0:n], in1=w_[1][:, 0:n], op=ALU.add)),
                         reads=[wr[0], wr[1]], writes=[wr[0]])
                    P.op("vector", ("tensor_tensor", _c(out=o[:, 0:n], in0=w_[0][:, 0:n], in1=w_[2][:, 0:n], op=ALU.mult)),
                         reads=[wr[0], wr[2]], writes=[ores])
                elif m in (QB, KB):
                    sw = NSW + m
                    b2 = bank_rr.next()
                    proj(sw * 128, b2)
                    w_ = wk[m % 2]
                    wr = [("wk", m % 2, k) for k in range(4)]
                    P.op("vector", ("tensor_tensor", _c(out=w_[0][:, 0:n], in0=pb[b1][:, 0:n], in1=tb[:, 2, 0:n], op=ALU.mult)),
                         reads=[PB(b1), tres], writes=[wr[0]])
                    P.op("vector", ("tensor_tensor", _c(out=w_[1][:, 0:n], in0=pb[b2][:, 0:n], in1=tb[:, 3, 0:n], op=ALU.mult)),
                         reads=[PB(b2), tres], writes=[wr[1]])
                    P.op("vector", ("tensor_tensor", _c(out=o[:, 0:n], in0=w_[0][:, 0:n], in1=w_[1][:, 0:n], op=ALU.add)),
                         reads=[wr[0], wr[1]], writes=[ores])
                elif m >= GG:
                    P.op("scalar", ("activation", _c(out=o[:, 0:n], in_=pb[b1][:, 0:n], func=AF.Silu)),
                         reads=[PB(b1)], writes=[ores])
                else:
                    copy_rr(o[:, 0:n], pb[b1][:, 0:n], [PB(b1)], [ores])
                store(m, o, ores)
            for tt in range(n // 128):
                vs = vst[tt % 2]
                vres = ("vst", tt % 2)
                for half in range(1):
                    b = bank_rr.next()
                    for kt in range(8):
                        P.op("tensor", ("matmul", _c(
                            pb[b][:, 0:320], lhsT=h[:, kt, tt * 128:(tt + 1) * 128],
                            rhs=wbf[:, kt, 1920:2240], start=(kt == 0), stop=(kt == 7))),
                            reads=wres + [hres], writes=[PB(b)])
                    copy_rr(vs[:, 0:320], pb[b][:, 0:320], [PB(b)], [vres])
                P.dma("gpsimd", f"vst{tt % 2}", vallv[:, c0 // 128 + tt, :], vs, reads=[vres], writes=[("vall", c0 // 128 + tt)])

    def load_head(kT, qT, gT, vaug, krow, qrow, grow, vcol):
        if kT is not None:
            P.dma("sync", "kT", kT[0:64, :], fT[krow:krow + 64, :], reads=["kT"], writes=["kT"])
            vv = vall.rearrange("(tt p) d -> p tt d", p=128)
            P.dma("sync", "vaug", vaug[:, 0:17, 0:64], vv[:, 0:17, vcol:vcol + 64], reads=["vaug"], writes=["vaug"])
            P.dma("sync", "vaug", vaug[:, 17:34, 0:64], vv[:, 17:34, vcol:vcol + 64], reads=["vaug"], writes=["vaug"])
        if qT is not None:
            P.dma("sync", "qT", qT[0:64, :], fT[qrow:qrow + 64, :], reads=["qT"], writes=["qT"])
        P.dma("sync", "gT", gT[0:64, :], fT[grow:grow + 64, :], writes=["gT"])

    def attn_pairs(kT, qT, vaug, c0, n, pairs, sp_rr, pT_rr, scale):
        first, last_ = {}, {}
        for i, pr in enumerate(pairs):
            for j, (_, _, _, accb) in enumerate(pr):
                first.setdefault(accb, (i, j))
                last_[accb] = (i, j)

        def smm(pr):
            sp = sp_rr.next()
            for j, (qt_, qres, kt, accb) in enumerate(pr):
                P.op("tensor", ("matmul", _c(pb[2 * sp + j][:, 0:n], lhsT=kT[:, kt * 128:(kt + 1) * 128],
                                             rhs=qt_[:, c0:c0 + n], start=True, stop=True)),
                     reads=["kT", qres], writes=[PB(2 * sp + j)])
            return sp
        queue = [smm(pr) for pr in pairs[0:2]]
        for i, pr in enumerate(pairs):
            sp = queue.pop(0)
            if i + 2 < len(pairs):
                queue.append(smm(pairs[i + 2]))
            pi, pT = pT_rr.next()
            P.op("scalar", ("activation", _c(out=pT.rearrange("p (b c) -> p b c", b=2)[:, :, 0:n],
                                             in_=pp[sp].rearrange("p (b c) -> p b c", b=2)[:, :, 0:n], func=AF.Exp, scale=scale)),
                 reads=[PB(2 * sp), PB(2 * sp + 1)], writes=[("pT", pi)])
            for j, (kp0, kp1, kt, accb) in enumerate(pr):
                P.op("tensor", ("matmul", _c(pb[accb][:, 0:n], lhsT=vaug[:, kt, :], rhs=pT[:, j * 512:j * 512 + n],
                                             start=(first[accb] == (i, j)), stop=(last_[accb] == (i, j)))),
                     reads=[("pT", pi), "vaug"], writes=[PB(accb)])

    def pairs_A(qT, kts, accb):
        kts = list(kts)
        return [[(qT, "qT", kts[i], accb), (qT, "qT", kts[i + 1], accb)] for i in range(0, len(kts), 2)]

    def pairs_B(qT, qT2, kts, a1, a2):
        return [[(qT, "qT", kt, a1), (qT2, "qT2", kt, a2)] for kt in kts]

    def finalize_simple(accb, gT, c0, n, rz, tf, o, ores, mrow):
        P.op("scalar", ("activation", _c(out=rz[64:128, 0:n], in_=pb[accb][64:128, 0:n], func=AF.Ln)), reads=[PB(accb)], writes=["rz"])
        P.op("scalar", ("activation", _c(out=rz[64:128, 0:n], in_=rz[64:128, 0:n], func=AF.Exp, scale=-1.0)), reads=["rz"], writes=["rz"])
        P.op("vector", ("tensor_tensor", _c(out=tf[0:64, 0:n], in0=pb[accb][0:64, 0:n], in1=rz[64:128, 0:n], op=ALU.mult)),
             reads=[PB(accb), "rz"], writes=["tf"])
        P.op("vector", ("tensor_tensor", _c(out=o[0:64, 0:n], in0=tf[0:64, 0:n], in1=gT[0:64, c0:c0 + n], op=ALU.mult)),
             reads=["tf", "gT"], writes=[ores])
        P.dma("gpsimd", f"ao{ores[1]}", mixL[mrow:mrow + 64, c0:c0 + n], o[0:64, 0:n], reads=[ores], writes=[("mixT", mrow, c0)])

    def attn_alloc():
        A32.reset()
        A16.reset()
        kT = A16.take(T)
        qT = A16.take(T)
        gT = A16.take(T)
        vaug = A16.take(NKT, 128)
        pTs = [A16.take(1024) for _ in range(4)]
        osts = [A16.take(512) for _ in range(2)]
        rz = A32.take(512)
        tf = A32.take(512)
        P.op("vector", ("memset", _c(vaug[:, :, 64:128], 1.0)), writes=["vaug"])
        P.op("vector", ("memset", _c(kT[64:128, :], 0.0)), writes=["kT"])
        P.op("vector", ("memset", _c(qT[:, :], 0.0)), writes=["qT"])
        return kT, qT, gT, vaug, pTs, osts, rz, tf

    def phase_A(l, last):
        kT, qT, gT, vaug, pTs, osts, rz, tf = attn_alloc()
        sp_rr = RR([0, 1, 2])
        acc_rr = RR([6, 7])
        pT_rr = RR(list(enumerate(pTs)))
        o_rr = RR(list(enumerate(osts)))
        for kvh in range(1):
            for hh in range(2):
                h = hh
                load_head(kT if hh == 0 else None, qT, gT, vaug, KA * 128, QA * 128 + h * 64, GG * 128 + h * 64, 0)
                for ci, (c0, n) in enumerate(CHUNKS):
                    if c0 >= NLAT and last:
                        continue
                    kts = range(NKT) if c0 < NLAT else (32, 33)
                    accb = acc_rr.next()
                    attn_pairs(kT, qT, vaug, c0, n, pairs_A(qT, kts, accb), sp_rr, pT_rr, 0.125)
                    oi, o = o_rr.next()
                    finalize_simple(accb, gT, c0, n, rz, tf, o, ("ost", oi), h * 64)

    def phase_B(l, last):
        kT, qT, gT, vaug, pTs, osts, rz, tf = attn_alloc()
        o1 = A32.take(512)
        rs = A32.take(512)
        rz2 = A32.take(512)
        sqb = A16.take(512)
        qT2 = A16.take(T)
        P.op("vector", ("memset", _c(qT2[:, :], 0.0)), writes=["qT2"])
        sp_rr = RR([0, 1, 2])
        acc_rr = RR([(6, 7)])
        pT_rr = RR(list(enumerate(pTs)))
        o_rr = RR(list(enumerate(osts)))
        sc_b = 32 ** -0.5
        for h in range(2):
            load_head(kT, None, gT, vaug, KB * 128 + h * 64, QB * 128 + h * 64, (GG + 1) * 128 + h * 64, 64 + h * 64)
            P.dma("sync", "qT", qT[0:32, :], fT[QB * 128 + h * 64:QB * 128 + h * 64 + 32, :], reads=["qT"], writes=["qT"])
            P.dma("sync", "qT2", qT2[32:64, :], fT[QB * 128 + h * 64 + 32:QB * 128 + h * 64 + 64, :], reads=["qT2"], writes=["qT2"])
            for ci, (c0, n) in enumerate(CHUNKS):
                if c0 >= NLAT and last:
                    continue
                kts = range(NKT) if c0 < NLAT else (32, 33)
                a1, a2 = acc_rr.next()
                attn_pairs(kT, qT, vaug, c0, n, pairs_B(qT, qT2, kts, a1, a2), sp_rr, pT_rr, sc_b)
                oi, o = o_rr.next()
                ores = ("ost", oi)
                P.op("scalar", ("activation", _c(out=rz[64:128, 0:n], in_=pb[a1][64:128, 0:n], func=AF.Ln)), reads=[PB(a1)], writes=["rz"])
                P.op("scalar", ("activation", _c(out=rz2[64:128, 0:n], in_=pb[a2][64:128, 0:n], func=AF.Ln)), reads=[PB(a2)], writes=["rz2"])
                P.op("scalar", ("activation", _c(out=rz[64:128, 0:n], in_=rz[64:128, 0:n], func=AF.Exp, scale=-1.0)), reads=["rz"], writes=["rz"])
                P.op("scalar", ("activation", _c(out=rz2[64:128, 0:n], in_=rz2[64:128, 0:n], func=AF.Exp, scale=-1.0)), reads=["rz2"], writes=["rz2"])
                P.op("vector", ("tensor_tensor", _c(out=o1[0:64, 0:n], in0=pb[a1][0:64, 0:n], in1=rz[64:128, 0:n], op=ALU.mult)),
                     reads=[PB(a1), "rz"], writes=["o1"])
                P.op("vector", ("tensor_tensor", _c(out=tf[0:64, 0:n], in0=pb[a2][0:64, 0:n], in1=rz2[64:128, 0:n], op=ALU.mult)),
                     reads=[PB(a2), "rz2"], writes=["tf"])
                P.op("vector", ("scalar_tensor_tensor", _c(out=o1[0:64, 0:n], in0=tf[0:64, 0:n], scalar=lams[0:64, 5:6],
                                                                in1=o1[0:64, 0:n], op0=ALU.mult, op1=ALU.add)),
                     reads=["tf", "o1", "lams"], writes=["o1"])
                P.op("vector", ("tensor_tensor", _c(out=sqb[0:64, 0:n], in0=o1[0:64, 0:n], in1=o1[0:64, 0:n], op=ALU.mult)), reads=["o1"], writes=["sqb"])
                sbk = 2 * sp_rr.next()
                P.op("tensor", ("matmul", _c(pb[sbk][0:64, 0:n], lhsT=ones_bf[0:64, 0:64], rhs=sqb[0:64, 0:n], start=True, stop=True)),
                     reads=["sqb", "ones_bf"], writes=[PB(sbk)])
                P.op("scalar", ("activation", _c(out=rs[0:64, 0:n], in_=pb[sbk][0:64, 0:n], func=AF.Ln, scale=1.0 / 64.0, bias=epsb[0:64, 0:1])),
                     reads=[PB(sbk), "epsb"], writes=["rs"])
                P.op("scalar", ("activation", _c(out=rs[0:64, 0:n], in_=rs[0:64, 0:n], func=AF.Exp, scale=-0.5)), reads=["rs"], writes=["rs"])
                P.op("vector", ("tensor_tensor", _c(out=o1[0:64, 0:n], in0=o1[0:64, 0:n], in1=rs[0:64, 0:n], op=ALU.mult)),
                     reads=["o1", "rs"], writes=["o1"])
                P.op("vector", ("scalar_tensor_tensor", _c(out=o[0:64, 0:n], in0=o1[0:64, 0:n], scalar=lams[0:64, 6:7],
                                                                     in1=gT[0:64, c0:c0 + n], op0=ALU.mult, op1=ALU.mult)),
                     reads=["o1", "gT", "lams"], writes=[ores])
                P.dma("gpsimd", f"ao{oi}", mixL[128 + h * 64:128 + (h + 1) * 64, c0:c0 + n], o[0:64, 0:n], reads=[ores],
                      writes=[("mixT", 128 + h * 64, c0)])

    def na_tiles(Pq):
        if Pq == 0:
            return [0, 1, 2, 3], 0
        if Pq == 1:
            return [0, 1, 2, 3], 4
        if Pq == 30:
            return [28, 29, 30, 31], 13
        if Pq == 31:
            return [28, 29, 30, 31], 17
        return [Pq - 2, Pq - 1, Pq, Pq + 1, Pq + 2], 8

    def phase_C(l, last):
        kT, qT, gT, vaug, pTs, osts, rz, tf = attn_alloc()
        bias = A32.take(NV, 128)
        sbs = [A32.take(640) for _ in range(2)]
        pTc = [A16.take(896) for _ in range(3)]
        sb_rr = RR([0, 1, 2, 3, 4, 5])
        sp_rr = RR([0, 1, 2])
        acc_rr = RR([6, 7])
        pT_rr = RR(list(enumerate(pTs)))
        o_rr = RR(list(enumerate(osts)))
        for h in range(2):
            load_head(kT, qT, gT, vaug, KC * 128 + h * 64, QC * 128 + h * 64, (GG + 2) * 128 + h * 64, 192 + h * 64)
            P.dma("sync", "nab", bias, nabias[l, h].rearrange("p (v q) -> p v q", q=128), writes=["nab"])
            for grp in range(8):
                accb = acc_rr.next()
                for pi4 in range(4):
                    Pq = grp * 4 + pi4
                    kps, v0 = na_tiles(Pq)
                    nl = len(kps)
                    q0 = Pq * 128
                    bA = sb_rr.next()
                    bB = sb_rr.next()
                    for jx, kp in enumerate(kps):
                        bb, col = (bA, jx * 128) if jx < 4 else (bB, 0)
                        P.op("tensor", ("matmul", _c(
                            pb[bb][:, col:col + 128], lhsT=kT[:, kp * 128:(kp + 1) * 128], rhs=qT[:, q0:q0 + 128], start=True, stop=True)),
                            reads=["kT", "qT"], writes=[PB(bb)])
                    for jx, kp in enumerate((32, 33)):
                        col = 128 + jx * 128
                        P.op("tensor", ("matmul", _c(
                            pb[bB][:, col:col + 128], lhsT=kT[:, kp * 128:(kp + 1) * 128], rhs=qT[:, q0:q0 + 128], start=True, stop=True)),
                            reads=["kT", "qT"], writes=[PB(bB)])
                    si = Pq % 2
                    sbt = sbs[si]
                    pci = Pq % 3
                    pT = pTc[pci]
                    P.op("vector", ("scalar_tensor_tensor", _c(
                        out=sbt[:, 0:512], in0=pb[bA][:, 0:512], scalar=0.125,
                        in1=bias[:, v0:v0 + 4, :].rearrange("p v q -> p (v q)"), op0=ALU.mult, op1=ALU.add)),
                        reads=[PB(bA), "nab"], writes=[("sbs", si)])
                    if nl == 5:
                        P.op("vector", ("scalar_tensor_tensor", _c(
                            out=sbt[:, 512:640], in0=pb[bB][:, 0:128], scalar=0.125, in1=bias[:, v0 + 4, :], op0=ALU.mult, op1=ALU.add)),
                            reads=[PB(bB), "nab", ("sbs", si)], writes=[("sbs", si)])
                    P.op("scalar", ("activation", _c(out=pT[:, 0:nl * 128], in_=sbt[:, 0:nl * 128], func=AF.Exp)),
                         reads=[("sbs", si)], writes=[("pTc", pci)])
                    P.op("scalar", ("activation", _c(out=pT[:, 640:896], in_=pb[bB][:, 128:384], func=AF.Exp, scale=0.125)),
                         reads=[PB(bB), ("pTc", pci)], writes=[("pTc", pci)])
                    tiles = [(kp, jx * 128) for jx, kp in enumerate(kps)] + [(32, 640), (33, 768)]
                    for ti, (kp, col) in enumerate(tiles):
                        P.op("tensor", ("matmul", _c(
                            pb[accb][:, pi4 * 128:(pi4 + 1) * 128], lhsT=vaug[:, kp, :], rhs=pT[:, col:col + 128],
                            start=(ti == 0), stop=(ti == len(tiles) - 1))), reads=[("pTc", pci), "vaug"], writes=[PB(accb)])
                oi, o = o_rr.next()
                finalize_simple(accb, gT, grp * 512, 512, rz, tf, o, ("ost", oi), 256 + h * 64)
            if not last:
                accb = acc_rr.next()
                attn_pairs(kT, qT, vaug, NLAT, NCTX, pairs_A(qT, (32, 33), accb), sp_rr, pT_rr, 0.125)
                oi, o = o_rr.next()
                finalize_simple(accb, gT, NLAT, NCTX, rz, tf, o, ("ost", oi), 256 + h * 64)

    def angle_tables(eng_src, out_sin, out_cos, b_ap, n, wres, tmp, tmpi, rtag):
        P.op("vector", ("tensor_copy", _c(out=tmpi, in_=b_ap)), reads=[rtag + "b"], writes=[rtag + "i"])
        P.op("vector", ("tensor_copy", _c(out=tmp, in_=tmpi)), reads=[rtag + "i"], writes=[rtag + "t"])
        P.op("vector", ("tensor_tensor", _c(out=b_ap, in0=b_ap, in1=tmp, op=ALU.subtract)), reads=[rtag + "b", rtag + "t"], writes=[rtag + "b"])
        P.op("scalar", ("activation", _c(out=out_sin, in_=b_ap, func=AF.Sin, scale=2.0 * math.pi)), reads=[rtag + "b"], writes=wres[0:1])
        P.op("vector", ("tensor_scalar", _c(out=b_ap, in0=b_ap, scalar1=0.25, scalar2=None, op0=ALU.add)),
             reads=[rtag + "b"] + wres[0:1], writes=[rtag + "b"])
        P.op("vector", ("tensor_copy", _c(out=tmpi, in_=b_ap)), reads=[rtag + "b"], writes=[rtag + "i"])
        P.op("vector", ("tensor_copy", _c(out=tmp, in_=tmpi)), reads=[rtag + "i"], writes=[rtag + "t"])
        P.op("vector", ("tensor_tensor", _c(out=b_ap, in0=b_ap, in1=tmp, op=ALU.subtract)), reads=[rtag + "b", rtag + "t"], writes=[rtag + "b"])
        P.op("scalar", ("activation", _c(out=out_cos, in_=b_ap, func=AF.Sin, scale=2.0 * math.pi)), reads=[rtag + "b"], writes=wres[1:2])

    def phase_D(l, last):
        A32.reset()
        A16.reset()
        V = "vector"
        pl = A32.take(48)
        plw = A32.take(10, 16)
        plwi = i32a[:, 0:16]
        tabS = A32.take(8, 512)
        tabC = A32.take(8, 512)
        rot = A32.take(16, 128)
        yf = A32.take(T)
        carry = A32.take(16, 2)
        W1 = A16.take(16, 128)
        W2 = A16.take(16, 128)
        W3 = A16.take(16, 128)
        W4 = A16.take(16, 128)
        uT = A16.take(1, T)
        mark32, mark16 = A32.off, A16.off
        P.dma("sync", "pl", pl, s5pl[l], writes=["pl"])
        P.dma("sync", "uT", uT, fT.rearrange("(m p) t -> p m t", p=128)[:, UU:UU + 1, :], writes=["uT"])
        are, aim, ldt = pl[:, 0:16], pl[:, 16:32], pl[:, 32:48]
        P.op("scalar", ("activation", _c(out=plw[:, 0, :], in_=ldt, func=AF.Exp)), reads=["pl"], writes=["plw0"])
        P.op(V, ("tensor_tensor", _c(out=plw[:, 1, :], in0=are, in1=plw[:, 0, :], op=ALU.mult)), reads=["pl", "plw0"], writes=["plw1"])
        P.op("scalar", ("activation", _c(out=plw[:, 1, :], in_=plw[:, 1, :], func=AF.Exp)), reads=["plw1"], writes=["plw1"])
        P.op(V, ("tensor_tensor", _c(out=plw[:, 2, :], in0=aim, in1=plw[:, 0, :], op=ALU.mult)), reads=["pl", "plw0"], writes=["plw2"])
        P.op(V, ("tensor_scalar", _c(out=plw[:, 2, :], in0=plw[:, 2, :], scalar1=1.0 / (2.0 * math.pi), scalar2=None, op0=ALU.mult)),
             reads=["plw2"], writes=["plw2"])
        P.op(V, ("tensor_scalar", _c(out=plw[:, 3, :], in0=plw[:, 2, :], scalar1=16384.0, scalar2=None, op0=ALU.add)), reads=["plw2"], writes=["plw3"])
        P.op(V, ("tensor_scalar", _c(out=plw[:, 3, :], in0=plw[:, 3, :], scalar1=-16384.0, scalar2=None, op0=ALU.add)), reads=["plw3"], writes=["plw3"])
        P.op(V, ("tensor_tensor", _c(out=plw[:, 4, :], in0=plw[:, 2, :], in1=plw[:, 3, :], op=ALU.subtract)), reads=["plw2", "plw3"], writes=["plw4"])
        P.op(V, ("tensor_scalar", _c(out=plw[:, 5, :], in0=plw[:, 3, :], scalar1=256.0, scalar2=None, op0=ALU.mult)), reads=["plw3"], writes=["r256b"])
        P.op(V, ("tensor_copy", _c(out=plwi, in_=plw[:, 5, :])), reads=["r256b"], writes=["r256i"])
        P.op(V, ("tensor_copy", _c(out=plw[:, 8, :], in_=plwi)), reads=["r256i"], writes=["r256t"])
        P.op(V, ("tensor_tensor", _c(out=plw[:, 5, :], in0=plw[:, 5, :], in1=plw[:, 8, :], op=ALU.subtract)), reads=["r256b", "r256t"], writes=["r256b"])
        P.op(V, ("scalar_tensor_tensor", _c(out=plw[:, 5, :], in0=plw[:, 4, :], scalar=256.0, in1=plw[:, 5, :], op0=ALU.mult, op1=ALU.add)),
             reads=["plw4", "r256b"], writes=["r256b"])
        angle_tables(None, plw[:, 6, :], plw[:, 7, :], plw[:, 5, :], 16, ["plw6", "plw7"], plw[:, 8, :], plwi, "r256")
        P.op(V, ("tensor_scalar", _c(out=plw[:, 9, :], in0=plw[:, 6, :], scalar1=sgn, scalar2=None, op0=ALU.mult)),
             reads=["plw6", "cst"], writes=["plw9"])

        rowv = A32.take(3, 512)
        kk = [A32.take(512) for _ in range(7)]
        ktmp = A32.take(512)
        kki = i32b[:, 0:512]
        bst = A32.take(2, 512)
        for pc in range(2):
            cs = slice(pc * 512, (pc + 1) * 512)
            P.dma("sync", "rowv", rowv, s5row[l, :, cs].partition_broadcast(128), writes=["rowv"])
            P.dma("sync", "bst", bst, s5b[l, :, :, cs].rearrange("k p n -> p k n"), writes=["bst"])
            ar, ai, ld = rowv[:, 0, :], rowv[:, 1, :], rowv[:, 2, :]
            dt_, e_, th, cs_, sn_, nr, den = kk
            P.op("scalar", ("activation", _c(out=dt_, in_=ld, func=AF.Exp)), reads=["rowv"], writes=["k0"])
            P.op(V, ("tensor_tensor", _c(out=e_, in0=ar, in1=dt_, op=ALU.mult)), reads=["rowv", "k0"], writes=["k1"])
            P.op("scalar", ("activation", _c(out=e_, in_=e_, func=AF.Exp)), reads=["k1"], writes=["k1"])
            P.op(V, ("tensor_tensor", _c(out=th, in0=ai, in1=dt_, op=ALU.mult)), reads=["rowv", "k0"], writes=["kab"])
            P.op(V, ("tensor_scalar", _c(out=th, in0=th, scalar1=1.0 / (2.0 * math.pi), scalar2=None, op0=ALU.mult)), reads=["kab"], writes=["kab"])
            angle_tables(None, sn_, cs_, th, 512, ["k4", "k3"], ktmp, kki, "ka")
            P.op(V, ("tensor_tensor", _c(out=cs_, in0=cs_, in1=e_, op=ALU.mult)), reads=["k3", "k1"], writes=["k3"])
            P.op(V, ("tensor_tensor", _c(out=sn_, in0=sn_, in1=e_, op=ALU.mult)), reads=["k4", "k1"], writes=["k4"])
            P.op(V, ("tensor_scalar", _c(out=nr, in0=cs_, scalar1=-1.0, scalar2=None, op0=ALU.add)), reads=["k3"], writes=["k5"])
            P.op(V, ("tensor_tensor", _c(out=den, in0=ar, in1=ar, op=ALU.mult)), reads=["rowv"], writes=["k6"])
            P.op(V, ("tensor_tensor", _c(out=th, in0=ai, in1=ai, op=ALU.mult)), reads=["rowv", "kab"], writes=["kab"])
            P.op(V, ("tensor_tensor", _c(out=den, in0=den, in1=th, op=ALU.add)), reads=["k6", "kab"], writes=["k6"])
            P.op(V, ("reciprocal", _c(out=den, in_=den)), reads=["k6"], writes=["k6"])
            P.op(V, ("tensor_tensor", _c(out=cs_, in0=nr, in1=ar, op=ALU.mult)), reads=["k5", "rowv", "k3"], writes=["k3"])
            P.op(V, ("tensor_tensor", _c(out=th, in0=sn_, in1=ai, op=ALU.mult)), reads=["k4", "rowv", "kab"], writes=["kab"])
            P.op(V, ("tensor_tensor", _c(out=cs_, in0=cs_, in1=th, op=ALU.add)), reads=["k3", "kab"], writes=["k3"])
            P.op(V, ("tensor_tensor", _c(out=cs_, in0=cs_, in1=den, op=ALU.mult)), reads=["k3", "k6"], writes=["k3"])
            P.op(V, ("tensor_tensor", _c(out=e_, in0=sn_, in1=ar, op=ALU.mult)), reads=["k4", "rowv", "k1"], writes=["k1"])
            P.op(V, ("tensor_tensor", _c(out=th, in0=nr, in1=ai, op=ALU.mult)), reads=["k5", "rowv", "kab"], writes=["kab"])
            P.op(V, ("tensor_tensor", _c(out=e_, in0=e_, in1=th, op=ALU.subtract)), reads=["k1", "kab"], writes=["k1"])
            P.op(V, ("tensor_tensor", _c(out=e_, in0=e_, in1=den, op=ALU.mult)), reads=["k1", "k6"], writes=["k1"])
            kre, kim = cs_, e_
            bre, bim = bst[:, 0, :], bst[:, 1, :]
            P.op(V, ("tensor_tensor", _c(out=sn_, in0=kre, in1=bre, op=ALU.mult)), reads=["k3", "bst", "k4"], writes=["k4"])
            P.op(V, ("tensor_tensor", _c(out=th, in0=kim, in1=bim, op=ALU.mult)), reads=["k1", "bst", "kab"], writes=["kab"])
            P.op(V, ("tensor_tensor", _c(out=sn_, in0=sn_, in1=th, op=ALU.subtract)), reads=["k4", "kab"], writes=["k4"])
            P.op(V, ("tensor_tensor", _c(out=nr, in0=kre, in1=bim, op=ALU.mult)), reads=["k3", "bst", "k5"], writes=["k5"])
            P.op(V, ("tensor_tensor", _c(out=th, in0=kim, in1=bre, op=ALU.mult)), reads=["k1", "bst", "kab"], writes=["kab"])
            P.op(V, ("tensor_tensor", _c(out=nr, in0=nr, in1=th, op=ALU.add)), reads=["k5", "kab"], writes=["k5"])
            Bre = sn_.rearrange("p (a b) -> p a b", b=64)
            Bim = nr.rearrange("p (a b) -> p a b", b=64)
            dgs = slice(pc * 8, (pc + 1) * 8)
            P.op(V, ("tensor_copy", _c(out=W1[:, dgs, 0:64], in_=Bre)), reads=["k4"], writes=["W1"])
            P.op(V, ("tensor_copy", _c(out=W1[:, dgs, 64:128], in_=Bim)), reads=["k5", "W1"], writes=["W1"])
            P.op(V, ("tensor_copy", _c(out=W2[:, dgs, 0:64], in_=Bim)), reads=["k5"], writes=["W2"])
            P.op(V, ("tensor_scalar", _c(out=W2[:, dgs, 64:128], in0=Bre, scalar1=-1.0, scalar2=None, op0=ALU.mult)), reads=["k4", "W2"], writes=["W2"])
        P.barrier()
        A32.off = mark32
        cstg = A32.take(2048)
        P.dma("sync", "cstg", cstg, s5c[l, 0], writes=["cstg"])
        P.op(V, ("tensor_scalar", _c(out=W3.rearrange("p a b -> p (a b)"), in0=cstg, scalar1=sgn, scalar2=None, op0=ALU.mult)),
             reads=["cstg", "cst"], writes=["W3"])
        P.dma("sync", "cstg", cstg, s5c[l, 1], reads=["cstg"], writes=["cstg"])
        P.op(V, ("tensor_scalar", _c(out=W4.rearrange("p a b -> p (a b)"), in0=cstg, scalar1=-1.0, scalar2=None, op0=ALU.mult)),
             reads=["cstg"], writes=["W4"])
        P.barrier()
        A32.off = mark32
        t1 = [A32.take(512) for _ in range(3)]
        t2 = [A32.take(512) for _ in range(3)]
        G = [A32.take(512) for _ in range(3)]
        tb_ = A32.take(512)
        tbi = i32c[:, 0:512]
        tmpb = A32.take(512)
        gl = A32.take(16, 2)
        A1 = [A16.take(512) for _ in range(3)]
        A2 = [A16.take(512) for _ in range(3)]
        ge_all = A16.take(1, T)
        gd = [A16.take(2, 512) for _ in range(2)]
        wg = A16.take(2, 256)
        wgs = A32.take(2, 256)
        ost = [A16.take(512) for _ in range(2)]
        gq = A32.take(512)
        gsg = A32.take(512)
        gdg = A16.take(512)
        P.dma("sync", "wgs", wgs, w_glu[l].rearrange("(kt p) n -> p kt n", p=128), writes=["wgs"])
        P.op(V, ("tensor_copy", _c(out=wg, in_=wgs)), reads=["wgs"], writes=["wg"])
        order_f = [8] + list(range(8))
        order_b = [8] + list(range(7, -1, -1))
        it = 0
        for half in range(1):
            for d in range(2):
                iota = iota_f if d == 0 else iota_r
                for gi in range(8):
                    dg = d * 8 + gi
                    rt = f"tb{gi}"
                    P.op(V, ("tensor_scalar", _c(out=tb_, in0=iota, scalar1=plw[:, 3, dg:dg + 1], scalar2=None, op0=ALU.mult)),
                         reads=["cst", "plw3"], writes=["tbb"])
                    P.op(V, ("tensor_copy", _c(out=tbi, in_=tb_)), reads=["tbb"], writes=["tbi"])
                    P.op(V, ("tensor_copy", _c(out=tmpb, in_=tbi)), reads=["tbi"], writes=["tbt"])
                    P.op(V, ("tensor_tensor", _c(out=tb_, in0=tb_, in1=tmpb, op=ALU.subtract)), reads=["tbb", "tbt"], writes=["tbb"])
                    P.op(V, ("scalar_tensor_tensor", _c(out=tb_, in0=iota, scalar=plw[:, 4, dg:dg + 1], in1=tb_, op0=ALU.mult, op1=ALU.add)),
                         reads=["cst", "plw4", "tbb"], writes=["tbb"])
                    angle_tables(None, tabS[:, gi, :], tabC[:, gi, :], tb_, 512, [("tabS", gi), ("tabC", gi)], tmpb, tbi, "tb")
                    ri = d * 8 + gi
                    P.op(V, ("tensor_scalar", _c(out=rot[:, ri, :], in0=ident, scalar1=plw[:, 7, dg:dg + 1], scalar2=None, op0=ALU.mult)),
                         reads=["cst", "plw7"], writes=[("rot", ri)])
                    P.op(V, ("scalar_tensor_tensor", _c(out=rot[:, ri, :], in0=shift64, scalar=plw[:, 9, dg:dg + 1], in1=rot[:, ri, :],
                                                                        op0=ALU.mult, op1=ALU.add)),
                         reads=["cst", "plw9", ("rot", ri)], writes=[("rot", ri)])
                    P.op(V, ("memset", _c(carry[:, gi + 8 * d, :], 0.0)), writes=[("carry", gi + 8 * d)])
                order = order_f if d == 0 else order_b
                steps = [(ci, gi) for ci in order for gi in range(8)]
                NS = len(steps)

                def geom(k):
                    ci, gi = steps[k]
                    c0, n = CHUNKS[ci]
                    tcs = slice(0, n) if (d == 0 or n == 512) else slice(256, 512)
                    return ci, gi, c0, n, tcs, d * 8 + gi, k % 3, ((0, 1) if k % 2 == 0 else (2, 3))

                def stage_a(k):
                    ci, gi, c0, n, tcs, dg, s, (b1, b2) = geom(k)
                    P.op("tensor", ("matmul", _c(pb[b1][:, 0:n], lhsT=W1[:, dg, :], rhs=uT[:, half, c0:c0 + n], start=True, stop=True)),
                         reads=["W1", "uT"], writes=[PB(b1)])
                    P.op("tensor", ("matmul", _c(pb[b2][:, 0:n], lhsT=W2[:, dg, :], rhs=uT[:, half, c0:c0 + n], start=True, stop=True)),
                         reads=["W2", "uT"], writes=[PB(b2)])
                    P.op(V, ("tensor_tensor", _c(out=t1[s][:, 0:n], in0=pb[b1][:, 0:n], in1=tabC[:, gi, tcs], op=ALU.mult)),
                         reads=[PB(b1), ("tabC", gi)], writes=[("t1", s)])
                    P.op(V, ("tensor_tensor", _c(out=t2[s][:, 0:n], in0=pb[b2][:, 0:n], in1=tabS[:, gi, tcs], op=ALU.mult)),
                         reads=[PB(b2), ("tabS", gi)], writes=[("t2", s)])
                    P.op(V, ("tensor_tensor", _c(out=t1[s][:, 0:n], in0=t1[s][:, 0:n], in1=t2[s][:, 0:n], op=ALU.add)),
                         reads=[("t1", s), ("t2", s)], writes=[("t1", s)])

                def stage_b(k):
                    ci, gi, c0, n, tcs, dg, s, _ = geom(k)
                    cres = ("carry", gi + 8 * d)
                    rcol = plw[:, 1, dg:dg + 1]
                    if d == 0:
                        g_out, g_in = G[s][:, 0:n], t1[s][:, 0:n]
                        glast = G[s][:, n - 1:n]
                    else:
                        g_out, g_in = G[s][:, 0:n][:, ::-1], t1[s][:, 0:n][:, ::-1]
                        glast = G[s][:, 0:1]
                    P.op(V, ("tensor_tensor_scan", _c(
                        out=g_out, data0=rcol.to_broadcast([128, n]), data1=g_in, initial=carry[:, gi + 8 * d, 0:1], op0=ALU.mult, op1=ALU.add)),
                        reads=[("t1", s), "plw1", cres], writes=[("G", s)])
                    P.op("scalar", ("copy", _c(out=gl[:, gi + 8 * d, :], in_=glast.to_broadcast([128, 2]))),
                         reads=[("G", s)], writes=[("gl", gi + 8 * d)])
                    P.op("gpsimd", ("tensor_tensor", _c(out=A1[s][:, 0:n], in0=G[s][:, 0:n], in1=tabC[:, gi, tcs], op=ALU.mult)),
                         reads=[("G", s), ("tabC", gi)], writes=[("A1", s)])
                    P.op("gpsimd", ("tensor_tensor", _c(out=A2[s][:, 0:n], in0=G[s][:, 0:n], in1=tabS[:, gi, tcs], op=ALU.mult)),
                         reads=[("G", s), ("tabS", gi)], writes=[("A2", s)])

                def stage_c(k):
                    ci, gi, c0, n, tcs, dg, s, _ = geom(k)
                    cres = ("carry", gi + 8 * d)
                    gres = ("gl", gi + 8 * d)
                    ybank = 6 + ((k // 8) % 2)
                    P.op("tensor", ("matmul", _c(pb[ybank][:, 0:n], lhsT=W3[:, dg, :], rhs=A1[s][:, 0:n], start=(gi == 0), stop=False)),
                         reads=["W3", ("A1", s)], writes=[PB(ybank)])
                    P.op("tensor", ("matmul", _c(pb[ybank][:, 0:n], lhsT=W4[:, dg, :], rhs=A2[s][:, 0:n], start=False, stop=(gi == 7))),
                         reads=["W4", ("A2", s)], writes=[PB(ybank)])
                    for kk_ in range(n // 256):
                        P.op("tensor", ("matmul", _c(pb[4][:, 2 * gi:2 * gi + 2], lhsT=rot[:, gi + 8 * d, :], rhs=gl[:, gi + 8 * d, :], start=True, stop=True)),
                             reads=[("rot", gi + 8 * d), gres], writes=[("pb4", gi)])
                        dst = gl if kk_ + 1 < n // 256 else carry
                        dres = gres if kk_ + 1 < n // 256 else cres
                        P.op("scalar", ("copy", _c(out=dst[:, gi + 8 * d, :], in_=pb[4][:, 2 * gi:2 * gi + 2])),
                             reads=[("pb4", gi)], writes=[dres])
                    if gi != 7:
                        return
                    if d == 0:
                        copy_rr(yf[:, c0:c0 + n], pb[ybank][:, 0:n], [PB(ybank)], [("yf", ci)])
                    else:
                        P.op(V, ("tensor_tensor", _c(out=yf[:, c0:c0 + n], in0=pb[ybank][:, 0:n], in1=yf[:, c0:c0 + n], op=ALU.add)),
                             reads=[PB(ybank), ("yf", ci)], writes=[("yf", ci)])
                        P.op(V, ("scalar_tensor_tensor", _c(out=yf[:, c0:c0 + n], in0=uT[:, half, c0:c0 + n], scalar=pv[:, 5:6],
                                                            in1=yf[:, c0:c0 + n], op0=ALU.mult, op1=ALU.add)),
                             reads=["uT", "pv", ("yf", ci)], writes=[("yf", ci)])
                        P.op("scalar", ("activation", _c(out=gq[:, 0:n], in_=yf[:, c0:c0 + n], func=AF.Square)), reads=[("yf", ci)], writes=["gq"])
                        P.op(V, ("tensor_scalar", _c(out=gq[:, 0:n], in0=gq[:, 0:n], scalar1=0.044715, scalar2=1.0, op0=ALU.mult, op1=ALU.add)),
                             reads=["gq"], writes=["gq"])
                        P.op(V, ("tensor_tensor", _c(out=gq[:, 0:n], in0=gq[:, 0:n], in1=yf[:, c0:c0 + n], op=ALU.mult)), reads=["gq", ("yf", ci)], writes=["gq"])
                        P.op("scalar", ("activation", _c(out=gq[:, 0:n], in_=gq[:, 0:n], func=AF.Sigmoid, scale=2.0 * math.sqrt(2.0 / math.pi))),
                             reads=["gq"], writes=["gq"])
                        P.op(V, ("tensor_tensor", _c(out=ge_all[:, half, c0:c0 + n], in0=gq[:, 0:n], in1=yf[:, c0:c0 + n], op=ALU.mult)),
                             reads=["gq", ("yf", ci)], writes=[("ge", half, ci)])

                for tck in range(NS + 2):
                    if tck < NS:
                        stage_a(tck)
                    if 0 <= tck - 1 < NS:
                        stage_b(tck - 1)
                    if 0 <= tck - 2 < NS:
                        stage_c(tck - 2)
        P.dma("gpsimd", "geL", geL, ge_all[:, 0, :], reads=[("ge", 0, ci_) for ci_ in range(9)], writes=["geL"])
        P.collective("ccg", "AllGather", ALU.bypass, PAIRS, geL, geF, reads=["geL"], writes=["geF"])
        fTv = fT.rearrange("(m p) t -> p m t", p=128)
        geFv = geF.rearrange("(kt p) t -> p kt t", p=128)
        gin_ = [A16.take(2, 512) for _ in range(2)]
        bank_rr = RR([0, 1, 2, 3])
        for ci, (c0, n) in enumerate(CHUNKS):
            if last and c0 >= NLAT:
                continue
            gdt = gd[ci % 2]
            P.dma("sync", f"gd{ci % 2}", gdt[:, 0, 0:n], fTv[:, GG + 3, c0:c0 + n], writes=[("gd", ci % 2)])
            gi_ = gin_[ci % 2]
            P.dma("sync", f"gin{ci % 2}", gi_[:, :, 0:n], geFv[:, :, c0:c0 + n], reads=["geF"], writes=[("gin", ci % 2)])
            bv = bank_rr.next()
            bg = bank_rr.next()
            for kt in range(2):
                P.op("tensor", ("matmul", _c(pb[bv][:, 0:n], lhsT=wg[:, kt, 0:128], rhs=gi_[:, kt, 0:n], start=(kt == 0), stop=(kt == 1))),
                     reads=["wg", ("gin", ci % 2)], writes=[PB(bv)])
            for kt in range(2):
                P.op("tensor", ("matmul", _c(pb[bg][:, 0:n], lhsT=wg[:, kt, 128:256], rhs=gi_[:, kt, 0:n], start=(kt == 0), stop=(kt == 1))),
                     reads=["wg", ("gin", ci % 2)], writes=[PB(bg)])
            P.op("scalar", ("activation", _c(out=gsg[:, 0:n], in_=pb[bg][:, 0:n], func=AF.Sigmoid)), reads=[PB(bg)], writes=["gsg"])
            P.op(V, ("tensor_tensor", _c(out=gsg[:, 0:n], in0=pb[bv][:, 0:n], in1=gsg[:, 0:n], op=ALU.mult)), reads=[PB(bv), "gsg"], writes=["gsg"])
            o = ost[ci % 2]
            P.op(V, ("tensor_tensor", _c(out=o[:, 0:n], in0=gsg[:, 0:n], in1=gdt[:, 0, 0:n], op=ALU.mult)),
                 reads=["gsg", ("gd", ci % 2)], writes=[("dost", ci % 2)])
            P.dma("gpsimd", f"dost{ci % 2}", mixL[384:512, c0:c0 + n], o[:, 0:n], reads=[("dost", ci % 2)],
                  writes=[("mixT", 384, c0)])

    def phase_out(l, last):
        A32.reset()
        A16.reset()
        wst = [A32.take(8, 256) for _ in range(2)]
        xk = [A32.take(512) for _ in range(3)]
        vbuf = [A32.take(8, 512) for _ in range(2)]
        st = [A32.take(512) for _ in range(4)]
        tq = [A32.take(512) for _ in range(2)]
        ostage = [A32.take(512) for _ in range(4)]
        wbf = A16.take(8, 1024)
        mx = [A16.take(8, 512) for _ in range(2)]
        vb = [A16.take(512) for _ in range(2)]
        qb = [A16.take(512) for _ in range(2)]
        wv = w_out[l].rearrange("(kt p) n -> p kt n", p=128)
        for piece in range(4):
            w = wst[piece % 2]
            cs = slice(piece * 256, (piece + 1) * 256)
            P.dma("sync", f"wst{piece % 2}", w, wv[:, :, cs], writes=[("wst", piece % 2)])
            copy_rr(wbf[:, :, cs], w, [("wst", piece % 2)], [("wbf", piece)])
        wres = [("wbf", p_) for p_ in range(4)]
        mixv = mixT.rearrange("(kt p) t -> p kt t", p=128)
        bank_rr = RR([0, 1, 2, 3])
        for ci, (c0, n) in enumerate(CHUNKS):
            if last and c0 >= NLAT:
                continue
            j = 0 if c0 < NLAT else 1
            m = mx[ci % 2]
            mres = ("mx", ci % 2)
            v = vbuf[ci % 2]
            vp = ci % 2
            P.dma("sync", f"mx{ci % 2}", m[:, :, 0:n], mixv[:, :, c0:c0 + n], writes=[mres])
            for ft in range(8):
                b = bank_rr.next()
                for kt in range(8):
                    P.op("tensor", ("matmul", _c(pb[b][:, 0:n], lhsT=wbf[:, kt, ft * 128:(ft + 1) * 128], rhs=m[:, kt, 0:n],
                                                                       start=(kt == 0), stop=(kt == 7))), reads=wres + [mres], writes=[PB(b)])
                xb = xk[ft % 3]
                xres = ("xk", ft % 3)
                P.dma("sync", f"xk{ft % 3}", xb[:, 0:n], xTv[:, ft, c0:c0 + n], reads=[("xT", ci)], writes=[xres])
                P.op("scalar", ("mul", _c(out=xb[:, 0:n], in_=xb[:, 0:n], mul=ALPHA)), reads=[xres], writes=[xres])
                vres = ("v", vp, ft)
                P.op("vector", ("scalar_tensor_tensor", _c(out=v[:, ft, 0:n], in0=pb[b][:, 0:n], scalar=modsb[:, 16 + ft, j:j + 1],
                                                                                in1=xb[:, 0:n], op0=ALU.mult, op1=ALU.add)),
                     reads=[PB(b), xres, "modsb"], writes=[vres])
                vbb = vb[ft % 2]
                qbb = qb[ft % 2]
                P.op("vector", ("tensor_copy", _c(out=vbb[:, 0:n], in_=v[:, ft, 0:n])), reads=[vres], writes=[("vb", ft % 2)])
                P.op("scalar", ("activation", _c(out=qbb[:, 0:n], in_=v[:, ft, 0:n], func=AF.Square)), reads=[vres], writes=[("qb", ft % 2)])
                P.op("tensor", ("matmul", _c(pb[4][:, 0:n], lhsT=ones_bf[:], rhs=vbb[:, 0:n], start=(ft == 0), stop=(ft == 7))),
                     reads=[("vb", ft % 2), "ones_bf"], writes=[PB(4)])
                P.op("tensor", ("matmul", _c(pb[5][:, 0:n], lhsT=ones_bf[:], rhs=qbb[:, 0:n], start=(ft == 0), stop=(ft == 7))),
                     reads=[("qb", ft % 2), "ones_bf"], writes=[PB(5)])
            mean, var, rstd, tmp = st
            P.op("vector", ("tensor_scalar", _c(out=mean[:, 0:n], in0=pb[4][:, 0:n], scalar1=1.0 / 1024.0, scalar2=None, op0=ALU.mult)),
                 reads=[PB(4)], writes=["mean"])
            P.op("vector", ("tensor_tensor", _c(out=tmp[:, 0:n], in0=mean[:, 0:n], in1=mean[:, 0:n], op=ALU.mult)), reads=["mean"], writes=["tmp"])
            P.op("vector", ("scalar_tensor_tensor", _c(out=var[:, 0:n], in0=pb[5][:, 0:n], scalar=1.0 / 1024.0, in1=tmp[:, 0:n],
                                                            op0=ALU.mult, op1=ALU.subtract)), reads=[PB(5), "tmp"], writes=["var"])
            P.op("scalar", ("activation", _c(out=rstd[:, 0:n], in_=var[:, 0:n], func=AF.Sqrt, scale=1.0, bias=LN_EPS)), reads=["var"], writes=["rstd"])
            P.op("vector", ("reciprocal", _c(out=rstd[:, 0:n], in_=rstd[:, 0:n])), reads=["rstd"], writes=["rstd"])
            for ft in range(8):
                vres = ("v", vp, ft)
                t_ = tq[ft % 2]
                tres = ("tq", ft % 2)
                P.op("vector", ("tensor_tensor", _c(out=t_[:, 0:n], in0=v[:, ft, 0:n], in1=mean[:, 0:n], op=ALU.subtract)),
                     reads=[vres, "mean"], writes=[tres])
                P.op("vector", ("tensor_tensor", _c(out=t_[:, 0:n], in0=t_[:, 0:n], in1=rstd[:, 0:n], op=ALU.mult)),
                     reads=[tres, "rstd"], writes=[tres])
                P.op("scalar", ("activation", _c(out=v[:, ft, 0:n], in_=t_[:, 0:n], func=AF.Identity,
                                                                   bias=pv[:, 15 + ft:16 + ft], scale=pv[:, 7 + ft:8 + ft])),
                     reads=[tres, "pv", vres], writes=[vres])
            if not last:
                P.dma("gpsimd", f"xst{vp}", xTv[:, :, c0:c0 + n], v[:, :, 0:n], reads=[("v", vp, f) for f in range(8)], writes=[("xT", ci)])
            else:
                for tt in range(n // 128):
                    for half in range(2):
                        b = bank_rr.next()
                        for f4 in range(4):
                            ft = half * 4 + f4
                            P.op("tensor", ("transpose", _c(out=pb[b][:, f4 * 128:(f4 + 1) * 128],
                                                                                       in_=v[:, ft, tt * 128:(tt + 1) * 128], identity=ident)),
                                 reads=[("v", vp, ft), "cst"], writes=[PB(b)])
                        og = ostage[(tt * 2 + half) % 4]
                        ogr = ("ostg", (tt * 2 + half) % 4)
                        copy_rr(og[:, 0:512], pb[b][:, 0:512], [PB(b)], [ogr])
                        P.dma("gpsimd", f"ostg{(tt * 2 + half) % 4}", out[c0 + tt * 128:c0 + (tt + 1) * 128, half * 512:(half + 1) * 512], og[:, 0:512],
                              reads=[ogr], writes=[("out", c0, tt, half)])

    def phase_D_full(l, last):
        phase_D(l, last)

    phase0()
    P.barrier()
    for l in range(n_layers):
        last = (l == DEPTH - 1)
        phase_ada(l)
        P.barrier()
        phase_in(l)
        P.barrier()
        phase_A(l, last)
        P.barrier()
        phase_B(l, last)
        P.barrier()
        phase_C(l, last)
        P.barrier()
        for kb in range(3):
            P.collective(f"ccm{kb}", "AllGather", ALU.bypass, PAIRS, mixL[kb * 128:(kb + 1) * 128, :], mixT[kb * 256:(kb + 1) * 256, :])
        phase_D_full(l, last)
        P.barrier()
        for kb in range(3, 4):
            P.collective(f"ccm{kb}", "AllGather", ALU.bypass, PAIRS, mixL[kb * 128:(kb + 1) * 128, :], mixT[kb * 256:(kb + 1) * 256, :])
        P.barrier()
        phase_out(l, last)
        P.barrier()
    P.emit()
    return nc


IN_SIZES = (256, 128, 128, 256, 256, 256, 256, 256, 256, 256, 256, 256, 256, 256)
_OFF = np.concatenate([[0], np.cumsum(IN_SIZES)]).astype(int)
(_AQ, _AK, _AV, _AG, _BQ, _BK, _BV, _BG, _CQ, _CK, _CV, _CG, _DU, _DG) = [int(v) for v in _OFF[:-1]]


def _partner_A(d):
    return d + 16 if (d % 32) < 16 else d - 16


def _partner_B(e):
    return e + 8 if (e % 16) < 8 else e - 8


def _col_index(rank):
    r = lambda a, n: list(range(a, a + n))
    rk = rank
    akv = r(_AK + rk * 64, 64) + r(_AK + (1 - rk) * 64, 64)
    main = (r(_AQ + rk * 128, 128) + akv + r(_BQ + rk * 128, 128) + r(_BK + rk * 128, 128)
            + r(_CQ + rk * 128, 128) + r(_CK + rk * 128, 128) + r(_DU + rk * 128, 128)
            + r(_AG + rk * 128, 128) + r(_BG + rk * 128, 128) + r(_CG + rk * 128, 128) + r(_DG + rk * 128, 128))
    swap = []
    for h in range(2):
        swap += [_AQ + rk * 128 + h * 64 + _partner_A(d) for d in range(64)]
    for hv in (rk, 1 - rk):
        swap += [_AK + hv * 64 + _partner_A(d) for d in range(64)]
    for base in (_BQ, _BK):
        for blk in range(4):
            swap += [base + rk * 128 + blk * 32 + _partner_B(e) for e in range(32)]
    vcols = r(_AV + rk * 64, 64) + r(_BV + rk * 128, 128) + r(_CV + rk * 128, 128)
    idx = np.array(main + swap + vcols, dtype=np.int64)
    assert idx.shape[0] == NEXT, idx.shape
    return idx


def _rope_tables():
    f32 = np.float32
    t = np.arange(NLAT)
    row = (t // GRID_W).astype(f32)
    col = (t % GRID_W).astype(f32)
    tabs = np.zeros((4, 128, T), f32)
    tabs[0] = 1.0
    tabs[2] = 1.0
    for p in range(128):
        d = p % 64
        blk, j = d // 32, d % 32
        half = 16
        inv = f32(ROPE_BASE) ** (-(f32(j % half) / f32(half)))
        ang = (row if blk == 0 else col) * f32(inv)
        tabs[0, p, :NLAT] = np.cos(ang)
        tabs[1, p, :NLAT] = (-np.sin(ang)) if (j < half) else np.sin(ang)
        e = p % 32
        blk, j = e // 16, e % 16
        half = 8
        inv = f32(ROPE_BASE) ** (-(f32(j % half) / f32(half)))
        ang = (row if blk == 0 else col) * f32(inv)
        tabs[2, p, :NLAT] = np.cos(ang)
        tabs[3, p, :NLAT] = (-np.sin(ang)) if (j < half) else np.sin(ang)
    return tabs


def _na_gather_index():
    reps = [(0, k) for k in range(4)] + [(1, k) for k in range(4)] + [(2, k) for k in range(5)] \
        + [(30, k) for k in range(28, 32)] + [(31, k) for k in range(28, 32)]
    assert len(reps) == NV
    PAD = 15 * 31
    idx = np.full((NV, 128, 128), PAD, dtype=np.int64)
    w = np.arange(64)
    cs = np.clip(w - 8, 0, 48)
    for v, (Pq, kp) in enumerate(reps):
        for rl in range(2):
            r = 2 * Pq + rl
            rs = min(max(r - 4, 0), 56)
            for krl in range(2):
                kr = 2 * kp + krl
                if not (rs <= kr < rs + 8):
                    continue
                for kc in range(64):
                    ok = (cs <= kc) & (kc < cs + 16)
                    val = (kr - r + 7) * 31 + (kc - w + 15)
                    idx[v, krl * 64 + kc, rl * 64 + w[ok]] = val[ok]
    return idx


_CONST_CACHE = {}


def _constants():
    if "c" in _CONST_CACHE:
        return _CONST_CACHE["c"]
    consts = np.zeros((128, 1288), np.float32)
    consts[:, 0:128] = np.eye(128, dtype=np.float32)
    for k in range(128):
        consts[k, 128 + (k + 64) % 128] = 1.0
    consts[:, 256:768] = np.arange(512, dtype=np.float32)[None, :]
    consts[:, 768:1280] = np.arange(511, -1, -1, dtype=np.float32)[None, :]
    consts[:64, 1280] = 1.0
    consts[64:, 1280] = -1.0
    out = dict(consts=consts, rope=_rope_tables(), naidx=_na_gather_index())
    _CONST_CACHE["c"] = out
    return out


def _mix_perm():
    perm = []
    for rk in range(2):
        for base in (0, 256, 512, 768):
            perm += list(range(base + rk * 128, base + (rk + 1) * 128))
    return np.array(perm, dtype=np.int64)


def _prep_shared(inp):
    C = _constants()
    f32 = np.float32
    sh = {}
    sh["w_ada"] = np.ascontiguousarray(inp["w_ada"], dtype=f32)
    sh["w_out"] = np.ascontiguousarray(inp["w_out"], dtype=f32)
    lam = np.concatenate([inp["lam_q1"], inp["lam_k1"], inp["lam_q2"], inp["lam_k2"]], axis=1).astype(f32)
    sh["lamv"] = np.ascontiguousarray(np.broadcast_to(lam[:, None, :], (DEPTH, 128, 128)))
    sh["rope"] = C["rope"]
    sh["consts"] = C["consts"]
    return sh


def _prep_rank(inp, rk):
    C = _constants()
    f32 = np.float32
    L = DEPTH
    sh = {}
    sh["w_ext"] = np.ascontiguousarray(inp["w_in"][:, :, _col_index(rk)], dtype=f32)
    wg = inp["w_glu"]
    sh["w_glu"] = np.ascontiguousarray(np.concatenate([wg[:, :, rk * 128:(rk + 1) * 128], wg[:, :, 256 + rk * 128:256 + (rk + 1) * 128]], axis=2), dtype=f32)
    pvec = np.zeros((L, 128, NPV), f32)
    d64 = np.arange(128) % 64
    pa = np.array([_partner_A(int(d)) for d in d64])
    pvec[:, :, 0] = inp["qn_g"][:, d64]
    pvec[:, :, 1] = inp["qn_g"][:, pa]
    pvec[:, :, 2] = inp["kn_g"][:, d64]
    pvec[:, :, 3] = inp["kn_g"][:, pa]
    pvec[:, :, 4] = inp["subln_g"][:, d64]
    pvec[:, :, 5] = inp["s5_d"].reshape(L, 2, 128)[:, rk, :]
    pvec[:, :, 7:15] = inp["ln_g"].reshape(L, 8, 128).transpose(0, 2, 1)
    pvec[:, :, 15:23] = inp["ln_b"].reshape(L, 8, 128).transpose(0, 2, 1)
    pvec[:, :, 23:47] = inp["b_ada"].reshape(L, 24, 128).transpose(0, 2, 1)
    sh["pvec"] = pvec
    rp = inp["na_rpb"].reshape(L, 4, 15 * 31).astype(f32)[:, rk * 2:rk * 2 + 2]
    rp = np.concatenate([rp, np.full((L, 2, 1), -30000.0, f32)], axis=2)
    nb = rp[:, :, C["naidx"]]
    sh["nabias"] = np.ascontiguousarray(nb.transpose(0, 1, 3, 2, 4).reshape(L, 2, 128, NV * 128))
    gs = slice(rk * 8, rk * 8 + 8)
    are = inp["s5_a_re"].astype(f32)[:, :, gs]
    aim = inp["s5_a_im"].astype(f32)[:, :, gs]
    ldt = inp["s5_log_dt"].astype(f32)[:, :, gs]
    pl = np.zeros((L, 128, 48), f32)
    a1 = are.reshape(L, 16, 64).transpose(0, 2, 1)
    a2 = aim.reshape(L, 16, 64).transpose(0, 2, 1)
    pl[:, 0:64, 0:16] = a1
    pl[:, 64:128, 0:16] = a1
    pl[:, 0:64, 16:32] = a2
    pl[:, 64:128, 16:32] = a2
    pl[:, :, 32:48] = ldt.reshape(L, 1, 16)
    sh["s5pl"] = pl
    row = np.zeros((L, 3, 1024), f32)
    row[:, 0] = are.reshape(L, 1024)
    row[:, 1] = aim.reshape(L, 1024)
    row[:, 2] = np.repeat(ldt.reshape(L, 16), 64, axis=1)
    sh["s5row"] = row
    sb = np.zeros((L, 2, 128, 1024), f32)
    sc_ = np.zeros((L, 2, 128, 2048), f32)
    bre = inp["s5_b_re"].astype(f32)[:, :, gs]
    bim = inp["s5_b_im"].astype(f32)[:, :, gs]
    cre = inp["s5_c_re"].astype(f32)[:, :, gs]
    cim = inp["s5_c_im"].astype(f32)[:, :, gs]
    for d in range(2):
        for g in range(8):
            dg = d * 8 + g
            r0 = g * 16
            sb[:, 0, r0:r0 + 16, dg * 64:(dg + 1) * 64] = bre[:, d, g].transpose(0, 2, 1)
            sb[:, 1, r0:r0 + 16, dg * 64:(dg + 1) * 64] = bim[:, d, g].transpose(0, 2, 1)
            c0 = dg * 128 + r0
            sc_[:, 0, 0:64, c0:c0 + 16] = cre[:, d, g].transpose(0, 2, 1)
            sc_[:, 0, 64:128, c0:c0 + 16] = cim[:, d, g].transpose(0, 2, 1)
            sc_[:, 1, 0:64, c0:c0 + 16] = cim[:, d, g].transpose(0, 2, 1)
            sc_[:, 1, 64:128, c0:c0 + 16] = cre[:, d, g].transpose(0, 2, 1)
    sh["s5b"] = sb
    sh["s5c"] = sc_
    return sh


def _prep_core(inp, b):
    f32 = np.float32
    m = {}
    m["x_tok"] = np.ascontiguousarray(np.concatenate([inp["x"][b], inp["ctx"][b]], axis=0), dtype=f32)
    cv = np.zeros((128, 8, 2), f32)
    cv[:, :, 0] = inp["c"][b].reshape(8, 128).T
    cv[:, :, 1] = inp["c_ctx"].reshape(8, 128).T
    m["cvec"] = cv
    return m


_PROG_CACHE = {}


def _run(inputs, n_layers=DEPTH, debug=False):
    inp = {k: np.asarray(v) for k, v in inputs.items()}
    key = (n_layers, debug)
    nc = build_program(n_layers, debug)
    sh = _prep_shared(inp)
    rk = [_prep_rank(inp, 0), _prep_rank(inp, 1)]
    in_maps = []
    for core in range(8):
        m = dict(sh)
        m.update(rk[core % 2])
        m.update(_prep_core(inp, core // 2))
        in_maps.append(m)
    res = run_bass_kernel_spmd(nc, in_maps, core_ids=list(range(8)))
    return res


def kernel(**inputs):
    res = _run(inputs)
    out = np.stack([np.asarray(res.results[2 * b]["out"], dtype=np.float32) for b in range(4)], axis=0)
    return out
```

```python
import math
import numpy as np
import concourse.bass as bass
import concourse.mybir as mybir
from concourse.bass_utils import run_bass_kernel_spmd
from contextlib import ExitStack

F32 = mybir.dt.float32
BF16 = mybir.dt.bfloat16
I32 = mybir.dt.int32
ALU = mybir.AluOpType
AF = mybir.ActivationFunctionType

COMPUTE = ("tensor", "scalar", "vector", "gpsimd")
QUEUES = ("sync",)
ALLENG = COMPUTE + QUEUES

D_MODEL = 1024
DEPTH = 4
NLAT = 4096
NCTX = 256
T = NLAT + NCTX
GRID_W = 64
ROPE_BASE = 10000.0
RMS_EPS = 1e-6
LN_EPS = 1e-5
ALPHA = (2 * DEPTH) ** 0.25
CHUNKS = [(i * 512, 512) for i in range(8)] + [(NLAT, NCTX)]
NKT = T // 128
NEXT = 2240
QA, KA, QB, KB, QC, KC, UU, GG = 0, 1, 2, 3, 4, 5, 6, 7
NFT = 11
NSW = 11
NPV = 47
NV = 21


def _c(*a, **k):
    return (a, k)


class Op:
    __slots__ = ("eng", "fn", "waits", "signal", "count", "dma_sem", "dma_count", "idx", "dma_inc")

    def __init__(self, eng, fn):
        self.eng = eng
        self.fn = fn
        self.waits = {}
        self.signal = False
        self.count = None
        self.dma_sem = None
        self.dma_count = None
        self.dma_inc = 16


class Prog:
    def __init__(self, nc):
        self.nc = nc
        self.ops = {e: [] for e in ALLENG}
        self.last_write = {}
        self.readers = {}
        self.dma_sems = {}
        self.pending = {e: {} for e in ALLENG}
        self.stack = ExitStack()

    def sbuf(self, name, shape, dtype):
        return self.stack.enter_context(self.nc.sbuf_tensor(name, list(shape), dtype))

    def psum(self, name, shape, dtype=F32):
        return self.stack.enter_context(self.nc.psum_tensor(name, list(shape), dtype))

    @staticmethod
    def _gt(a, b):
        ia = a if isinstance(a, int) else a.idx
        ib = b if isinstance(b, int) else b.idx
        return ia > ib

    def _addwait(self, op, d):
        if d is op:
            return
        if d.dma_sem is not None:
            key, val = ("dma", d.dma_sem), d.dma_count
        else:
            if d.eng == "tensor" and op.eng == "tensor":
                return
            key, val = d.eng, d
        cur = op.waits.get(key)
        if cur is None or self._gt(val, cur):
            op.waits[key] = val

    def _deps(self, op, reads, writes):
        for k, v in self.pending[op.eng].items():
            cur = op.waits.get(k)
            if cur is None or self._gt(v, cur):
                op.waits[k] = v
        self.pending[op.eng] = {}
        for r in reads:
            w = self.last_write.get(r)
            if w is not None:
                self._addwait(op, w)
        for w_ in writes:
            w = self.last_write.get(w_)
            if w is not None:
                self._addwait(op, w)
            for rd in self.readers.get(w_, ()):
                self._addwait(op, rd)
        for r in reads:
            self.readers.setdefault(r, []).append(op)
        for w_ in writes:
            self.last_write[w_] = op
            self.readers[w_] = []

    def op(self, eng, fn, reads=(), writes=()):
        o = Op(eng, fn)
        o.idx = len(self.ops[eng])
        self._deps(o, reads, writes)
        self.ops[eng].append(o)
        return o

    def dma(self, queue, sem, out, in_, reads=(), writes=()):
        o = Op(queue, ("dma_start", _c(out=out, in_=in_)))
        o.idx = len(self.ops[queue])
        self._deps(o, reads, writes)
        c = self.dma_sems.get(sem, 0) + 16
        self.dma_sems[sem] = c
        o.dma_sem = sem
        o.dma_count = c
        self.ops[queue].append(o)
        return o

    def collective(self, sem, kind, op, groups, in_ap, out_ap, reads=(), writes=()):
        o = Op("gpsimd", ("collective_compute", _c(kind, op, replica_groups=groups, ins=[in_ap], outs=[out_ap])))
        o.idx = len(self.ops["gpsimd"])
        self._deps(o, reads, writes)
        c = self.dma_sems.get(sem, 0) + 1
        self.dma_sems[sem] = c
        o.dma_sem = sem
        o.dma_count = c
        o.dma_inc = 1
        self.ops["gpsimd"].append(o)
        return o

    def barrier(self):
        for e in ALLENG:
            pend = self.pending[e]
            for e2 in COMPUTE:
                nd = [o_ for o_ in self.ops[e2] if o_.dma_sem is None]
                if nd and not (e == "tensor" and e2 == "tensor"):
                    last = nd[-1]
                    cur = pend.get(e2)
                    if cur is None or self._gt(last, cur):
                        pend[e2] = last
            for s, c in self.dma_sems.items():
                pend[("dma", s)] = c
        self.last_write = {}
        self.readers = {}

    def emit(self):
        nc = self.nc
        for e in self.ops:
            for o in self.ops[e]:
                for k, v in o.waits.items():
                    if not isinstance(v, int):
                        v.signal = True
        for e in COMPUTE:
            c = 0
            for o in self.ops[e]:
                if o.signal and o.dma_sem is None:
                    c += 1
                    o.count = c
        with ExitStack() as st:
            sems = {}
            for e in COMPUTE:
                sems[e] = st.enter_context(nc.semaphore("s_" + e))
            for name in self.dma_sems:
                sems[("dma", name)] = st.enter_context(nc.semaphore("d_" + name))
            block = st.enter_context(nc.Block())
            prog = self

            def run(engname):
                def body(eng):
                    waited = {}
                    for o in prog.ops[engname]:
                        for k, v in o.waits.items():
                            val = v if isinstance(v, int) else v.count
                            if waited.get(k, 0) >= val:
                                continue
                            eng.wait_ge(sems[k], val)
                            waited[k] = val
                        if callable(o.fn):
                            ins = o.fn(eng)
                        else:
                            meth, (a_, k_) = o.fn
                            ins = getattr(eng, meth)(*a_, **k_)
                        if o.dma_sem is not None:
                            if o.dma_inc == 16:
                                ins.then_inc(sems[("dma", o.dma_sem)], 16)
                            else:
                                ins.then_inc(sems[("dma", o.dma_sem)])
                        elif o.signal:
                            ins.then_inc(sems[engname], 1)
                    if engname == "sync":
                        for name, tot in prog.dma_sems.items():
                            eng.wait_ge(sems[("dma", name)], tot)
                        for e2 in COMPUTE:
                            cnts = [o.count for o in prog.ops[e2] if o.count]
                            if cnts:
                                eng.wait_ge(sems[e2], cnts[-1])
                return body

            block.tensor(run("tensor"))
            block.scalar(run("scalar"))
            block.vector(run("vector"))
            block.gpsimd(run("gpsimd"))
            block.sync(run("sync"))
        self.stack.close()


class Arena:
    def __init__(self, ap, size):
        self.ap = ap
        self.size = size
        self.off = 0

    def reset(self):
        self.off = 0

    def take(self, *shape):
        n = int(np.prod(shape))
        assert self.off + n <= self.size, (self.off, n, self.size)
        a = self.ap[:, self.off:self.off + n]
        self.off += n
        if len(shape) == 2:
            return a.rearrange("p (a b) -> p a b", a=shape[0])
        if len(shape) == 3:
            return a.rearrange("p (a b c) -> p a b c", a=shape[0], b=shape[1])
        return a


class RR:
    def __init__(self, items):
        self.items = items
        self.i = 0

    def next(self):
        it = self.items[self.i % len(self.items)]
        self.i += 1
        return it


def build_program(n_layers=DEPTH, debug=False):
    nc = bass.Bass("TRN2", target_bir_lowering=False)
    P = Prog(nc)

    def din(name, shape, dt=F32):
        return nc.dram_tensor(name, list(shape), dt, kind="ExternalInput").ap()

    x_tok = din("x_tok", [T, 1024])
    cvec = din("cvec", [128, 8, 2])
    w_ada = din("w_ada", [DEPTH, 1024, 3072])
    w_ext = din("w_ext", [DEPTH, 1024, NEXT])
    w_out = din("w_out", [DEPTH, 1024, 1024])
    w_glu = din("w_glu", [DEPTH, 256, 256])
    pvec = din("pvec", [DEPTH, 128, NPV])
    lamv = din("lamv", [DEPTH, 128, 128])
    rope = din("rope", [4, 128, T])
    nabias = din("nabias", [DEPTH, 2, 128, NV * 128])
    s5pl = din("s5pl", [DEPTH, 128, 48])
    s5row = din("s5row", [DEPTH, 3, 1024])
    s5b = din("s5b", [DEPTH, 2, 128, 1024])
    s5c = din("s5c", [DEPTH, 2, 128, 2048])
    consts = din("consts", [128, 1288])
    out = nc.dram_tensor("out", [NLAT, 1024], F32, kind="ExternalOutput").ap()
    skind = "ExternalOutput" if debug else "Internal"
    xT = nc.dram_tensor("xT", [1024, T], F32, kind=skind).ap()
    fT = nc.dram_tensor("fT", [NFT * 128, T], BF16, kind=skind).ap()
    vall = nc.dram_tensor("vall", [T, 320], BF16, kind=skind).ap()
    mixL = nc.dram_tensor("mixL", [512, T], BF16, kind="Internal").ap()
    mixT = nc.dram_tensor("mixT", [1024, T], BF16, kind="Internal").ap()
    geL = nc.dram_tensor("geL", [128, T], BF16, kind="Internal").ap()
    geF = nc.dram_tensor("geF", [256, T], BF16, kind="Internal").ap()
    PAIRS = [[0, 1], [2, 3], [4, 5], [6, 7]]

    cst = P.sbuf("cst", [128, 1288], F32)
    ident = cst[:, 0:128]
    shift64 = cst[:, 128:256]
    iota_f = cst[:, 256:768]
    iota_r = cst[:, 768:1280]
    sgn = cst[:, 1280:1281]
    ones_bf = P.sbuf("ones_bf", [128, 128], BF16)
    blk_bf = P.sbuf("blk_bf", [128, 128], BF16)
    sc = P.sbuf("sc", [128, 8, 2], F32)
    pv = P.sbuf("pv", [128, NPV], F32)
    modsb = P.sbuf("modsb", [128, 24, 2], F32)
    sc1 = P.sbuf("sc1", [128, 8, 2], F32)
    lamt = P.sbuf("lamt", [128, 128], F32)
    lams = P.sbuf("lams", [128, 8], F32)
    epsb = P.sbuf("epsb", [128, 1], F32)
    i32a = P.sbuf("i32a", [128, 32], I32)
    i32b = P.sbuf("i32b", [128, 512], I32)
    i32c = P.sbuf("i32c", [128, 512], I32)
    a32t = P.sbuf("a32", [128, 23552], F32)
    a16t = P.sbuf("a16", [128, 46592], BF16)
    A32 = Arena(a32t, 23552)
    A16 = Arena(a16t, 46592)
    pp = [P.psum(f"pp{i}", [128, 1024], F32) for i in range(4)]
    pb = [pp[i // 2][:, (i % 2) * 512:(i % 2 + 1) * 512] for i in range(8)]

    def PB(i):
        return ("pb", i)

    P.dma("sync", "cst", cst[:], consts, writes=["cst"])
    P.op("vector", ("memset", _c(ones_bf[:], 1.0)), writes=["ones_bf"])
    P.op("vector", ("memset", _c(epsb[:], RMS_EPS)), writes=["epsb"])
    P.op("vector", ("memset", _c(blk_bf[:], 0.0)), writes=["blk_bf"])
    P.op("vector", ("memset", _c(blk_bf[0:64, 0:64], 1.0)), reads=["blk_bf"], writes=["blk_bf"])
    P.op("vector", ("memset", _c(blk_bf[64:128, 64:128], 1.0)), reads=["blk_bf"], writes=["blk_bf"])
    P.dma("sync", "sc", sc[:], cvec, writes=["sc"])
    P.op("scalar", ("activation", _c(out=sc[:], in_=sc[:], func=AF.Silu)), reads=["sc"], writes=["sc"])

    xTv = xT.rearrange("(ft p) t -> p ft t", p=128)
    cp_rr = [0]

    def copy_rr(out_ap, in_ap, reads, writes):
        cp_rr[0] += 1
        if cp_rr[0] % 2:
            return P.op("vector", ("tensor_copy", _c(out=out_ap, in_=in_ap)), reads=reads, writes=writes)
        return P.op("scalar", ("copy", _c(out=out_ap, in_=in_ap)), reads=reads, writes=writes)

    def phase0():
        A32.reset()
        xin = [A32.take(1024) for _ in range(4)]
        stage = A32.take(8, 512)
        for ci, (c0, n) in enumerate(CHUNKS):
            ntt = n // 128
            for tt in range(ntt):
                P.dma("sync", f"xin{tt}", xin[tt], x_tok[c0 + tt * 128:c0 + (tt + 1) * 128, :], writes=[("xin", tt)])
            for ft in range(8):
                for tt in range(ntt):
                    P.op("tensor", ("transpose", _c(out=pb[ft][:, tt * 128:(tt + 1) * 128],
                                                                   in_=xin[tt][:, ft * 128:(ft + 1) * 128], identity=ident)),
                         reads=[("xin", tt), "cst"], writes=[PB(ft)])
                copy_rr(stage[:, ft, 0:n], pb[ft][:, 0:n], [PB(ft)], [("xst", ft)])
            P.dma("gpsimd", "xst", xTv[:, :, c0:c0 + n], stage[:, :, 0:n], reads=[("xst", f) for f in range(8)],
                  writes=[("xT", ci)])

    def phase_ada(l):
        A32.reset()
        wst = [A32.take(8, 512) for _ in range(2)]
        P.dma("sync", "pv", pv[:], pvec[l], writes=["pv"])
        P.dma("sync", "lamt", lamt[:], lamv[l], writes=["lamt"])
        wv = w_ada[l].rearrange("(kt p) n -> p kt n", p=128)
        for piece in range(6):
            w = wst[piece % 2]
            P.dma("sync", f"wst{piece % 2}", w, wv[:, :, piece * 512:(piece + 1) * 512], writes=[("wst", piece % 2)])
            for j in range(4):
                ft = piece * 4 + j
                for kt in range(8):
                    P.op("tensor", ("matmul", _c(
                        pb[0][:, ft * 2:ft * 2 + 2], lhsT=w[:, kt, j * 128:(j + 1) * 128], rhs=sc[:, kt, :],
                        start=(kt == 0), stop=(kt == 7))), reads=[("wst", piece % 2), "sc"], writes=[PB(0)])
        P.op("vector", ("tensor_tensor", _c(
            out=modsb[:], in0=pb[0][:, 0:48].rearrange("p (a b) -> p a b", b=2),
            in1=pv[:, 23:47].unsqueeze(2).to_broadcast([128, 24, 2]), op=ALU.add)),
            reads=[PB(0), "pv"], writes=["modsb"])
        P.op("vector", ("tensor_scalar", _c(out=sc1[:], in0=modsb[:, 8:16, :], scalar1=1.0, scalar2=None, op0=ALU.add)),
             reads=["modsb"], writes=["sc1"])
        lam_init = 0.8 - 0.6 * math.exp(-0.3 * l)
        P.op("vector", ("tensor_tensor", _c(out=lamt[:, 0:32], in0=lamt[:, 0:32], in1=lamt[:, 32:64], op=ALU.mult)),
             reads=["lamt"], writes=["lamt"])
        P.op("vector", ("tensor_tensor", _c(out=lamt[:, 64:96], in0=lamt[:, 64:96], in1=lamt[:, 96:128], op=ALU.mult)),
             reads=["lamt"], writes=["lamt"])
        P.op("vector", ("reduce_sum", _c(out=lams[:, 0:1], in_=lamt[:, 0:32], axis=mybir.AxisListType.X)),
             reads=["lamt"], writes=["lams"])
        P.op("vector", ("reduce_sum", _c(out=lams[:, 1:2], in_=lamt[:, 64:96], axis=mybir.AxisListType.X)),
             reads=["lamt", "lams"], writes=["lams"])
        P.op("scalar", ("activation", _c(out=lams[:, 2:4], in_=lams[:, 0:2], func=AF.Exp)), reads=["lams"], writes=["lams"])
        P.op("vector", ("tensor_tensor", _c(out=lams[:, 4:5], in0=lams[:, 3:4], in1=lams[:, 2:3], op=ALU.subtract)),
             reads=["lams"], writes=["lams"])
        P.op("vector", ("tensor_scalar", _c(out=lams[:, 5:6], in0=lams[:, 4:5], scalar1=-lam_init, scalar2=None, op0=ALU.add)),
             reads=["lams"], writes=["lams"])
        P.op("vector", ("tensor_scalar", _c(out=lams[:, 6:7], in0=pv[:, 4:5], scalar1=(1.0 - lam_init), scalar2=None, op0=ALU.mult)),
             reads=["lams", "pv"], writes=["lams"])

    def phase_in(l):
        A32.reset()
        A16.reset()
        wst = [A32.take(8, 140) for _ in range(2)]
        xk = [A32.take(512) for _ in range(3)]
        tabs = [A32.take(4, 512) for _ in range(2)]
        wk = [[A32.take(512) for _ in range(4)] for _ in range(2)]
        wbf = A16.take(8, NEXT)
        hT = [A16.take(8, 512) for _ in range(2)]
        ost = [A16.take(512) for _ in range(4)]
        sqb = [A16.take(512) for _ in range(2)]
        vst = [A16.take(320) for _ in range(2)]
        wv = w_ext[l].rearrange("(kt p) n -> p kt n", p=128)
        for piece in range(16):
            w = wst[piece % 2]
            cs = slice(piece * 140, (piece + 1) * 140)
            P.dma("sync", f"wst{piece % 2}", w, wv[:, :, cs], writes=[("wst", piece % 2)])
            copy_rr(wbf[:, :, cs], w, [("wst", piece % 2)], [("wbf", piece)])
        wres = [("wbf", p_) for p_ in range(16)]
        bank_rr = RR(list(range(8)))
        ost_rr = RR(list(range(4)))
        fTv = fT.rearrange("(m p) t -> p m t", p=128)
        vallv = vall.rearrange("(tt p) d -> p tt d", p=128)
        for ci, (c0, n) in enumerate(CHUNKS):
            j = 0 if c0 < NLAT else 1
            h = hT[ci % 2]
            hres = ("hT", ci % 2)
            for kt in range(8):
                xb = xk[kt % 3]
                P.dma("sync", f"xk{kt % 3}", xb[:, 0:n], xTv[:, kt, c0:c0 + n], reads=[("xT", ci)], writes=[("xk", kt % 3)])
                P.op("scalar", ("activation", _c(
                    out=h[:, kt, 0:n], in_=xb[:, 0:n], func=AF.Identity, bias=modsb[:, kt, j:j + 1], scale=sc1[:, kt, j:j + 1])),
                    reads=[("xk", kt % 3), "modsb", "sc1"], writes=[hres])
            tb = tabs[ci % 2]
            P.dma("sync", f"tabs{ci % 2}", tb[:, :, 0:n], rope.rearrange("k p t -> p k t")[:, :, c0:c0 + n], writes=[("tabs", ci % 2)])
            tres = ("tabs", ci % 2)

            def proj(colbase, bank):
                for kt in range(8):
                    P.op("tensor", ("matmul", _c(pb[bank][:, 0:n], lhsT=wbf[:, kt, colbase:colbase + 128],
                                                             rhs=h[:, kt, 0:n], start=(kt == 0), stop=(kt == 7))),
                         reads=wres + [hres], writes=[PB(bank)])

            def store(m, o, ores):
                P.dma("gpsimd", f"ost{ores[1]}", fTv[:, m, c0:c0 + n], o[:, 0:n], reads=[ores], writes=[("fT", m, ci)])

            for m in range(NFT):
                b1 = bank_rr.next()
                proj(m * 128, b1)
                oi = ost_rr.next()
                o = ost[oi]
                ores = ("ost", oi)
                if m in (QA, KA):
                    sw = NSW + m
                    gcol = 0 if m == QA else 2
                    b2 = bank_rr.next()
                    proj(sw * 128, b2)
                    b3 = bank_rr.next()
                    w_ = wk[m % 2]
                    wr = [("wk", m % 2, k) for k in range(4)]
                    sq = sqb[m % 2]
                    P.op("scalar", ("activation", _c(out=sq[:, 0:n], in_=pb[b1][:, 0:n], func=AF.Square)),
                         reads=[PB(b1)], writes=[("sqb", m % 2)])
                    P.op("tensor", ("matmul", _c(pb[b3][:, 0:n], lhsT=blk_bf[:], rhs=sq[:, 0:n], start=True, stop=True)),
                         reads=[("sqb", m % 2), "blk_bf"], writes=[PB(b3)])
                    P.op("scalar", ("activation", _c(
                        out=w_[0][:, 0:n], in_=pb[b1][:, 0:n], func=AF.Identity, scale=pv[:, gcol:gcol + 1])),
                        reads=[PB(b1), "pv"], writes=[wr[0]])
                    P.op("scalar", ("activation", _c(
                        out=w_[1][:, 0:n], in_=pb[b2][:, 0:n], func=AF.Identity, scale=pv[:, gcol + 1:gcol + 2])),
                        reads=[PB(b2), "pv"], writes=[wr[1]])
                    P.op("scalar", ("activation", _c(
                        out=w_[2][:, 0:n], in_=pb[b3][:, 0:n], func=AF.Ln, scale=1.0 / 64.0, bias=epsb[:, 0:1])),
                        reads=[PB(b3), "epsb"], writes=[wr[2]])
                    P.op("scalar", ("activation", _c(out=w_[2][:, 0:n], in_=w_[2][:, 0:n], func=AF.Exp, scale=-0.5)), reads=[wr[2]], writes=[wr[2]])
                    P.op("vector", ("tensor_tensor", _c(out=w_[0][:, 0:n], in0=w_[0][:, 0:n], in1=tb[:, 0, 0:n], op=ALU.mult)),
                         reads=[wr[0], tres], writes=[wr[0]])
                    P.op("vector", ("tensor_tensor", _c(out=w_[1][:, 0:n], in0=w_[1][:, 0:n], in1=tb[:, 1, 0:n], op=ALU.mult)),
                         reads=[wr[1], tres], writes=[wr[1]])
                    P.op("vector", ("tensor_tensor", _c(out=w_[0][:, 0:n], in0=w_[0][:, 0:n], in1=w_[1][:, 0:n], op=ALU.add)),
                         reads=[wr[0], wr[1]], writes=[wr[0]])
                    P.op("vector", ("tensor_tensor", _c(out=o[:, 0:n], in0=w_[0][:, 0:n], in1=w_[2][:, 0:n], op=ALU.mult)),
                         reads=[wr[0], wr[2]], writes=[ores])
                elif m in (QB, KB):
                    sw = NSW + m
                    b2 = bank_rr.next()
                    proj(sw * 128, b2)
                    w_ = wk[m % 2]
                    wr = [("wk", m % 2, k) for k in range(4)]
                    P.op("vector", ("tensor_tensor", _c(out=w_[0][:, 0:n], in0=pb[b1][:, 0:n], in1=tb[:, 2, 0:n], op=ALU.mult)),
                         reads=[PB(b1), tres], writes=[wr[0]])
                    P.op("vector", ("tensor_tensor", _c(out=w_[1][:, 0:n], in0=pb[b2][:, 0:n], in1=tb[:, 3, 0:n], op=ALU.mult)),
                         reads=[PB(b2), tres], writes=[wr[1]])
                    P.op("vector", ("tensor_tensor", _c(out=o[:, 0:n], in0=w_[0][:, 0:n], in1=w_[1][:, 0:n], op=ALU.add)),
                         reads=[wr[0], wr[1]], writes=[ores])
                elif m >= GG:
                    P.op("scalar", ("activation", _c(out=o[:, 0:n], in_=pb[b1][:, 0:n], func=AF.Silu)),
                         reads=[PB(b1)], writes=[ores])
                else:
                    copy_rr(o[:, 0:n], pb[b1][:, 0:n], [PB(b1)], [ores])
                store(m, o, ores)
            for tt in range(n // 128):
                vs = vst[tt % 2]
                vres = ("vst", tt % 2)
                for half in range(1):
                    b = bank_rr.next()
                    for kt in range(8):
                        P.op("tensor", ("matmul", _c(
                            pb[b][:, 0:320], lhsT=h[:, kt, tt * 128:(tt + 1) * 128],
                            rhs=wbf[:, kt, 1920:2240], start=(kt == 0), stop=(kt == 7))),
                            reads=wres + [hres], writes=[PB(b)])
                    copy_rr(vs[:, 0:320], pb[b][:, 0:320], [PB(b)], [vres])
                P.dma("gpsimd", f"vst{tt % 2}", vallv[:, c0 // 128 + tt, :], vs, reads=[vres], writes=[("vall", c0 // 128 + tt)])

    def load_head(kT, qT, gT, vaug, krow, qrow, grow, vcol):
        if kT is not None:
            P.dma("sync", "kT", kT[0:64, :], fT[krow:krow + 64, :], reads=["kT"], writes=["kT"])
            vv = vall.rearrange("(tt p) d -> p tt d", p=128)
            P.dma("sync", "vaug", vaug[:, 0:17, 0:64], vv[:, 0:17, vcol:vcol + 64], reads=["vaug"], writes=["vaug"])
            P.dma("sync", "vaug", vaug[:, 17:34, 0:64], vv[:, 17:34, vcol:vcol + 64], reads=["vaug"], writes=["vaug"])
        if qT is not None:
            P.dma("sync", "qT", qT[0:64, :], fT[qrow:qrow + 64, :], reads=["qT"], writes=["qT"])
        P.dma("sync", "gT", gT[0:64, :], fT[grow:grow + 64, :], writes=["gT"])

    def attn_pairs(kT, qT, vaug, c0, n, pairs, sp_rr, pT_rr, scale):
        first, last_ = {}, {}
        for i, pr in enumerate(pairs):
            for j, (_, _, _, accb) in enumerate(pr):
                first.setdefault(accb, (i, j))
                last_[accb] = (i, j)

        def smm(pr):
            sp = sp_rr.next()
            for j, (qt_, qres, kt, accb) in enumerate(pr):
                P.op("tensor", ("matmul", _c(pb[2 * sp + j][:, 0:n], lhsT=kT[:, kt * 128:(kt + 1) * 128],
                                             rhs=qt_[:, c0:c0 + n], start=True, stop=True)),
                     reads=["kT", qres], writes=[PB(2 * sp + j)])
            return sp
        queue = [smm(pr) for pr in pairs[0:2]]
        for i, pr in enumerate(pairs):
            sp = queue.pop(0)
            if i + 2 < len(pairs):
                queue.append(smm(pairs[i + 2]))
            pi, pT = pT_rr.next()
            P.op("scalar", ("activation", _c(out=pT.rearrange("p (b c) -> p b c", b=2)[:, :, 0:n],
                                             in_=pp[sp].rearrange("p (b c) -> p b c", b=2)[:, :, 0:n], func=AF.Exp, scale=scale)),
                 reads=[PB(2 * sp), PB(2 * sp + 1)], writes=[("pT", pi)])
            for j, (kp0, kp1, kt, accb) in enumerate(pr):
                P.op("tensor", ("matmul", _c(pb[accb][:, 0:n], lhsT=vaug[:, kt, :], rhs=pT[:, j * 512:j * 512 + n],
                                             start=(first[accb] == (i, j)), stop=(last_[accb] == (i, j)))),
                     reads=[("pT", pi), "vaug"], writes=[PB(accb)])

    def pairs_A(qT, kts, accb):
        kts = list(kts)
        return [[(qT, "qT", kts[i], accb), (qT, "qT", kts[i + 1], accb)] for i in range(0, len(kts), 2)]

    def pairs_B(qT, qT2, kts, a1, a2):
        return [[(qT, "qT", kt, a1), (qT2, "qT2", kt, a2)] for kt in kts]

    def finalize_simple(accb, gT, c0, n, rz, tf, o, ores, mrow):
        P.op("scalar", ("activation", _c(out=rz[64:128, 0:n], in_=pb[accb][64:128, 0:n], func=AF.Ln)), reads=[PB(accb)], writes=["rz"])
        P.op("scalar", ("activation", _c(out=rz[64:128, 0:n], in_=rz[64:128, 0:n], func=AF.Exp, scale=-1.0)), reads=["rz"], writes=["rz"])
        P.op("vector", ("tensor_tensor", _c(out=tf[0:64, 0:n], in0=pb[accb][0:64, 0:n], in1=rz[64:128, 0:n], op=ALU.mult)),
             reads=[PB(accb), "rz"], writes=["tf"])
        P.op("vector", ("tensor_tensor", _c(out=o[0:64, 0:n], in0=tf[0:64, 0:n], in1=gT[0:64, c0:c0 + n], op=ALU.mult)),
             reads=["tf", "gT"], writes=[ores])
        P.dma("gpsimd", f"ao{ores[1]}", mixL[mrow:mrow + 64, c0:c0 + n], o[0:64, 0:n], reads=[ores], writes=[("mixT", mrow, c0)])

    def attn_alloc():
        A32.reset()
        A16.reset()
        kT = A16.take(T)
        qT = A16.take(T)
        gT = A16.take(T)
        vaug = A16.take(NKT, 128)
        pTs = [A16.take(1024) for _ in range(4)]
        osts = [A16.take(512) for _ in range(2)]
        rz = A32.take(512)
        tf = A32.take(512)
        P.op("vector", ("memset", _c(vaug[:, :, 64:128], 1.0)), writes=["vaug"])
        P.op("vector", ("memset", _c(kT[64:128, :], 0.0)), writes=["kT"])
        P.op("vector", ("memset", _c(qT[:, :], 0.0)), writes=["qT"])
        return kT, qT, gT, vaug, pTs, osts, rz, tf

    def phase_A(l, last):
        kT, qT, gT, vaug, pTs, osts, rz, tf = attn_alloc()
        sp_rr = RR([0, 1, 2])
        acc_rr = RR([6, 7])
        pT_rr = RR(list(enumerate(pTs)))
        o_rr = RR(list(enumerate(osts)))
        for kvh in range(1):
            for hh in range(2):
                h = hh
                load_head(kT if hh == 0 else None, qT, gT, vaug, KA * 128, QA * 128 + h * 64, GG * 128 + h * 64, 0)
                for ci, (c0, n) in enumerate(CHUNKS):
                    if c0 >= NLAT and last:
                        continue
                    kts = range(NKT) if c0 < NLAT else (32, 33)
                    accb = acc_rr.next()
                    attn_pairs(kT, qT, vaug, c0, n, pairs_A(qT, kts, accb), sp_rr, pT_rr, 0.125)
                    oi, o = o_rr.next()
                    finalize_simple(accb, gT, c0, n, rz, tf, o, ("ost", oi), h * 64)

    def phase_B(l, last):
        kT, qT, gT, vaug, pTs, osts, rz, tf = attn_alloc()
        o1 = A32.take(512)
        rs = A32.take(512)
        rz2 = A32.take(512)
        sqb = A16.take(512)
        qT2 = A16.take(T)
        P.op("vector", ("memset", _c(qT2[:, :], 0.0)), writes=["qT2"])
        sp_rr = RR([0, 1, 2])
        acc_rr = RR([(6, 7)])
        pT_rr = RR(list(enumerate(pTs)))
        o_rr = RR(list(enumerate(osts)))
        sc_b = 32 ** -0.5
        for h in range(2):
            load_head(kT, None, gT, vaug, KB * 128 + h * 64, QB * 128 + h * 64, (GG + 1) * 128 + h * 64, 64 + h * 64)
            P.dma("sync", "qT", qT[0:32, :], fT[QB * 128 + h * 64:QB * 128 + h * 64 + 32, :], reads=["qT"], writes=["qT"])
            P.dma("sync", "qT2", qT2[32:64, :], fT[QB * 128 + h * 64 + 32:QB * 128 + h * 64 + 64, :], reads=["qT2"], writes=["qT2"])
            for ci, (c0, n) in enumerate(CHUNKS):
                if c0 >= NLAT and last:
                    continue
                kts = range(NKT) if c0 < NLAT else (32, 33)
                a1, a2 = acc_rr.next()
                attn_pairs(kT, qT, vaug, c0, n, pairs_B(qT, qT2, kts, a1, a2), sp_rr, pT_rr, sc_b)
                oi, o = o_rr.next()
                ores = ("ost", oi)
                P.op("scalar", ("activation", _c(out=rz[64:128, 0:n], in_=pb[a1][64:128, 0:n], func=AF.Ln)), reads=[PB(a1)], writes=["rz"])
                P.op("scalar", ("activation", _c(out=rz2[64:128, 0:n], in_=pb[a2][64:128, 0:n], func=AF.Ln)), reads=[PB(a2)], writes=["rz2"])
                P.op("scalar", ("activation", _c(out=rz[64:128, 0:n], in_=rz[64:128, 0:n], func=AF.Exp, scale=-1.0)), reads=["rz"], writes=["rz"])
                P.op("scalar", ("activation", _c(out=rz2[64:128, 0:n], in_=rz2[64:128, 0:n], func=AF.Exp, scale=-1.0)), reads=["rz2"], writes=["rz2"])
                P.op("vector", ("tensor_tensor", _c(out=o1[0:64, 0:n], in0=pb[a1][0:64, 0:n], in1=rz[64:128, 0:n], op=ALU.mult)),
                     reads=[PB(a1), "rz"], writes=["o1"])
                P.op("vector", ("tensor_tensor", _c(out=tf[0:64, 0:n], in0=pb[a2][0:64, 0:n], in1=rz2[64:128, 0:n], op=ALU.mult)),
                     reads=[PB(a2), "rz2"], writes=["tf"])
                P.op("vector", ("scalar_tensor_tensor", _c(out=o1[0:64, 0:n], in0=tf[0:64, 0:n], scalar=lams[0:64, 5:6],
                                                                in1=o1[0:64, 0:n], op0=ALU.mult, op1=ALU.add)),
                     reads=["tf", "o1", "lams"], writes=["o1"])
                P.op("vector", ("tensor_tensor", _c(out=sqb[0:64, 0:n], in0=o1[0:64, 0:n], in1=o1[0:64, 0:n], op=ALU.mult)), reads=["o1"], writes=["sqb"])
                sbk = 2 * sp_rr.next()
                P.op("tensor", ("matmul", _c(pb[sbk][0:64, 0:n], lhsT=ones_bf[0:64, 0:64], rhs=sqb[0:64, 0:n], start=True, stop=True)),
                     reads=["sqb", "ones_bf"], writes=[PB(sbk)])
                P.op("scalar", ("activation", _c(out=rs[0:64, 0:n], in_=pb[sbk][0:64, 0:n], func=AF.Ln, scale=1.0 / 64.0, bias=epsb[0:64, 0:1])),
                     reads=[PB(sbk), "epsb"], writes=["rs"])
                P.op("scalar", ("activation", _c(out=rs[0:64, 0:n], in_=rs[0:64, 0:n], func=AF.Exp, scale=-0.5)), reads=["rs"], writes=["rs"])
                P.op("vector", ("tensor_tensor", _c(out=o1[0:64, 0:n], in0=o1[0:64, 0:n], in1=rs[0:64, 0:n], op=ALU.mult)),
                     reads=["o1", "rs"], writes=["o1"])
                P.op("vector", ("scalar_tensor_tensor", _c(out=o[0:64, 0:n], in0=o1[0:64, 0:n], scalar=lams[0:64, 6:7],
                                                                     in1=gT[0:64, c0:c0 + n], op0=ALU.mult, op1=ALU.mult)),
                     reads=["o1", "gT", "lams"], writes=[ores])
                P.dma("gpsimd", f"ao{oi}", mixL[128 + h * 64:128 + (h + 1) * 64, c0:c0 + n], o[0:64, 0:n], reads=[ores],
                      writes=[("mixT", 128 + h * 64, c0)])

    def na_tiles(Pq):
        if Pq == 0:
            return [0, 1, 2, 3], 0
        if Pq == 1:
            return [0, 1, 2, 3], 4
        if Pq == 30:
            return [28, 29, 30, 31], 13
        if Pq == 31:
            return [28, 29, 30, 31], 17
        return [Pq - 2, Pq - 1, Pq, Pq + 1, Pq + 2], 8

    def phase_C(l, last):
        kT, qT, gT, vaug, pTs, osts, rz, tf = attn_alloc()
        bias = A32.take(NV, 128)
        sbs = [A32.take(640) for _ in range(2)]
        pTc = [A16.take(896) for _ in range(3)]
        sb_rr = RR([0, 1, 2, 3, 4, 5])
        sp_rr = RR([0, 1, 2])
        acc_rr = RR([6, 7])
        pT_rr = RR(list(enumerate(pTs)))
        o_rr = RR(list(enumerate(osts)))
        for h in range(2):
            load_head(kT, qT, gT, vaug, KC * 128 + h * 64, QC * 128 + h * 64, (GG + 2) * 128 + h * 64, 192 + h * 64)
            P.dma("sync", "nab", bias, nabias[l, h].rearrange("p (v q) -> p v q", q=128), writes=["nab"])
            for grp in range(8):
                accb = acc_rr.next()
                for pi4 in range(4):
                    Pq = grp * 4 + pi4
                    kps, v0 = na_tiles(Pq)
                    nl = len(kps)
                    q0 = Pq * 128
                    bA = sb_rr.next()
                    bB = sb_rr.next()
                    for jx, kp in enumerate(kps):
                        bb, col = (bA, jx * 128) if jx < 4 else (bB, 0)
                        P.op("tensor", ("matmul", _c(
                            pb[bb][:, col:col + 128], lhsT=kT[:, kp * 128:(kp + 1) * 128], rhs=qT[:, q0:q0 + 128], start=True, stop=True)),
                            reads=["kT", "qT"], writes=[PB(bb)])
                    for jx, kp in enumerate((32, 33)):
                        col = 128 + jx * 128
                        P.op("tensor", ("matmul", _c(
                            pb[bB][:, col:col + 128], lhsT=kT[:, kp * 128:(kp + 1) * 128], rhs=qT[:, q0:q0 + 128], start=True, stop=True)),
                            reads=["kT", "qT"], writes=[PB(bB)])
                    si = Pq % 2
                    sbt = sbs[si]
                    pci = Pq % 3
                    pT = pTc[pci]
                    P.op("vector", ("scalar_tensor_tensor", _c(
                        out=sbt[:, 0:512], in0=pb[bA][:, 0:512], scalar=0.125,
                        in1=bias[:, v0:v0 + 4, :].rearrange("p v q -> p (v q)"), op0=ALU.mult, op1=ALU.add)),
                        reads=[PB(bA), "nab"], writes=[("sbs", si)])
                    if nl == 5:
                        P.op("vector", ("scalar_tensor_tensor", _c(
                            out=sbt[:, 512:640], in0=pb[bB][:, 0:128], scalar=0.125, in1=bias[:, v0 + 4, :], op0=ALU.mult, op1=ALU.add)),
                            reads=[PB(bB), "nab", ("sbs", si)], writes=[("sbs", si)])
                    P.op("scalar", ("activation", _c(out=pT[:, 0:nl * 128], in_=sbt[:, 0:nl * 128], func=AF.Exp)),
                         reads=[("sbs", si)], writes=[("pTc", pci)])
                    P.op("scalar", ("activation", _c(out=pT[:, 640:896], in_=pb[bB][:, 128:384], func=AF.Exp, scale=0.125)),
                         reads=[PB(bB), ("pTc", pci)], writes=[("pTc", pci)])
                    tiles = [(kp, jx * 128) for jx, kp in enumerate(kps)] + [(32, 640), (33, 768)]
                    for ti, (kp, col) in enumerate(tiles):
                        P.op("tensor", ("matmul", _c(
                            pb[accb][:, pi4 * 128:(pi4 + 1) * 128], lhsT=vaug[:, kp, :], rhs=pT[:, col:col + 128],
                            start=(ti == 0), stop=(ti == len(tiles) - 1))), reads=[("pTc", pci), "vaug"], writes=[PB(accb)])
                oi, o = o_rr.next()
                finalize_simple(accb, gT, grp * 512, 512, rz, tf, o, ("ost", oi), 256 + h * 64)
            if not last:
                accb = acc_rr.next()
                attn_pairs(kT, qT, vaug, NLAT, NCTX, pairs_A(qT, (32, 33), accb), sp_rr, pT_rr, 0.125)
                oi, o = o_rr.next()
                finalize_simple(accb, gT, NLAT, NCTX, rz, tf, o, ("ost", oi), 256 + h * 64)

    def angle_tables(eng_src, out_sin, out_cos, b_ap, n, wres, tmp, tmpi, rtag):
        P.op("vector", ("tensor_copy", _c(out=tmpi, in_=b_ap)), reads=[rtag + "b"], writes=[rtag + "i"])
        P.op("vector", ("tensor_copy", _c(out=tmp, in_=tmpi)), reads=[rtag + "i"], writes=[rtag + "t"])
        P.op("vector", ("tensor_tensor", _c(out=b_ap, in0=b_ap, in1=tmp, op=ALU.subtract)), reads=[rtag + "b", rtag + "t"], writes=[rtag + "b"])
        P.op("scalar", ("activation", _c(out=out_sin, in_=b_ap, func=AF.Sin, scale=2.0 * math.pi)), reads=[rtag + "b"], writes=wres[0:1])
        P.op("vector", ("tensor_scalar", _c(out=b_ap, in0=b_ap, scalar1=0.25, scalar2=None, op0=ALU.add)),
             reads=[rtag + "b"] + wres[0:1], writes=[rtag + "b"])
        P.op("vector", ("tensor_copy", _c(out=tmpi, in_=b_ap)), reads=[rtag + "b"], writes=[rtag + "i"])
        P.op("vector", ("tensor_copy", _c(out=tmp, in_=tmpi)), reads=[rtag + "i"], writes=[rtag + "t"])
        P.op("vector", ("tensor_tensor", _c(out=b_ap, in0=b_ap, in1=tmp, op=ALU.subtract)), reads=[rtag + "b", rtag + "t"], writes=[rtag + "b"])
        P.op("scalar", ("activation", _c(out=out_cos, in_=b_ap, func=AF.Sin, scale=2.0 * math.pi)), reads=[rtag + "b"], writes=wres[1:2])

    def phase_D(l, last):
        A32.reset()
        A16.reset()
        V = "vector"
        pl = A32.take(48)
        plw = A32.take(10, 16)
        plwi = i32a[:, 0:16]
        tabS = A32.take(8, 512)
        tabC = A32.take(8, 512)
        rot = A32.take(16, 128)
        yf = A32.take(T)
        carry = A32.take(16, 2)
        W1 = A16.take(16, 128)
        W2 = A16.take(16, 128)
        W3 = A16.take(16, 128)
        W4 = A16.take(16, 128)
        uT = A16.take(1, T)
        mark32, mark16 = A32.off, A16.off
        P.dma("sync", "pl", pl, s5pl[l], writes=["pl"])
        P.dma("sync", "uT", uT, fT.rearrange("(m p) t -> p m t", p=128)[:, UU:UU + 1, :], writes=["uT"])
        are, aim, ldt = pl[:, 0:16], pl[:, 16:32], pl[:, 32:48]
        P.op("scalar", ("activation", _c(out=plw[:, 0, :], in_=ldt, func=AF.Exp)), reads=["pl"], writes=["plw0"])
        P.op(V, ("tensor_tensor", _c(out=plw[:, 1, :], in0=are, in1=plw[:, 0, :], op=ALU.mult)), reads=["pl", "plw0"], writes=["plw1"])
        P.op("scalar", ("activation", _c(out=plw[:, 1, :], in_=plw[:, 1, :], func=AF.Exp)), reads=["plw1"], writes=["plw1"])
        P.op(V, ("tensor_tensor", _c(out=plw[:, 2, :], in0=aim, in1=plw[:, 0, :], op=ALU.mult)), reads=["pl", "plw0"], writes=["plw2"])
        P.op(V, ("tensor_scalar", _c(out=plw[:, 2, :], in0=plw[:, 2, :], scalar1=1.0 / (2.0 * math.pi), scalar2=None, op0=ALU.mult)),
             reads=["plw2"], writes=["plw2"])
        P.op(V, ("tensor_scalar", _c(out=plw[:, 3, :], in0=plw[:, 2, :], scalar1=16384.0, scalar2=None, op0=ALU.add)), reads=["plw2"], writes=["plw3"])
        P.op(V, ("tensor_scalar", _c(out=plw[:, 3, :], in0=plw[:, 3, :], scalar1=-16384.0, scalar2=None, op0=ALU.add)), reads=["plw3"], writes=["plw3"])
        P.op(V, ("tensor_tensor", _c(out=plw[:, 4, :], in0=plw[:, 2, :], in1=plw[:, 3, :], op=ALU.subtract)), reads=["plw2", "plw3"], writes=["plw4"])
        P.op(V, ("tensor_scalar", _c(out=plw[:, 5, :], in0=plw[:, 3, :], scalar1=256.0, scalar2=None, op0=ALU.mult)), reads=["plw3"], writes=["r256b"])
        P.op(V, ("tensor_copy", _c(out=plwi, in_=plw[:, 5, :])), reads=["r256b"], writes=["r256i"])
        P.op(V, ("tensor_copy", _c(out=plw[:, 8, :], in_=plwi)), reads=["r256i"], writes=["r256t"])
        P.op(V, ("tensor_tensor", _c(out=plw[:, 5, :], in0=plw[:, 5, :], in1=plw[:, 8, :], op=ALU.subtract)), reads=["r256b", "r256t"], writes=["r256b"])
        P.op(V, ("scalar_tensor_tensor", _c(out=plw[:, 5, :], in0=plw[:, 4, :], scalar=256.0, in1=plw[:, 5, :], op0=ALU.mult, op1=ALU.add)),
             reads=["plw4", "r256b"], writes=["r256b"])
        angle_tables(None, plw[:, 6, :], plw[:, 7, :], plw[:, 5, :], 16, ["plw6", "plw7"], plw[:, 8, :], plwi, "r256")
        P.op(V, ("tensor_scalar", _c(out=plw[:, 9, :], in0=plw[:, 6, :], scalar1=sgn, scalar2=None, op0=ALU.mult)),
             reads=["plw6", "cst"], writes=["plw9"])

        rowv = A32.take(3, 512)
        kk = [A32.take(512) for _ in range(7)]
        ktmp = A32.take(512)
        kki = i32b[:, 0:512]
        bst = A32.take(2, 512)
        for pc in range(2):
            cs = slice(pc * 512, (pc + 1) * 512)
            P.dma("sync", "rowv", rowv, s5row[l, :, cs].partition_broadcast(128), writes=["rowv"])
            P.dma("sync", "bst", bst, s5b[l, :, :, cs].rearrange("k p n -> p k n"), writes=["bst"])
            ar, ai, ld = rowv[:, 0, :], rowv[:, 1, :], rowv[:, 2, :]
            dt_, e_, th, cs_, sn_, nr, den = kk
            P.op("scalar", ("activation", _c(out=dt_, in_=ld, func=AF.Exp)), reads=["rowv"], writes=["k0"])
            P.op(V, ("tensor_tensor", _c(out=e_, in0=ar, in1=dt_, op=ALU.mult)), reads=["rowv", "k0"], writes=["k1"])
            P.op("scalar", ("activation", _c(out=e_, in_=e_, func=AF.Exp)), reads=["k1"], writes=["k1"])
            P.op(V, ("tensor_tensor", _c(out=th, in0=ai, in1=dt_, op=ALU.mult)), reads=["rowv", "k0"], writes=["kab"])
            P.op(V, ("tensor_scalar", _c(out=th, in0=th, scalar1=1.0 / (2.0 * math.pi), scalar2=None, op0=ALU.mult)), reads=["kab"], writes=["kab"])
            angle_tables(None, sn_, cs_, th, 512, ["k4", "k3"], ktmp, kki, "ka")
            P.op(V, ("tensor_tensor", _c(out=cs_, in0=cs_, in1=e_, op=ALU.mult)), reads=["k3", "k1"], writes=["k3"])
            P.op(V, ("tensor_tensor", _c(out=sn_, in0=sn_, in1=e_, op=ALU.mult)), reads=["k4", "k1"], writes=["k4"])
            P.op(V, ("tensor_scalar", _c(out=nr, in0=cs_, scalar1=-1.0, scalar2=None, op0=ALU.add)), reads=["k3"], writes=["k5"])
            P.op(V, ("tensor_tensor", _c(out=den, in0=ar, in1=ar, op=ALU.mult)), reads=["rowv"], writes=["k6"])
            P.op(V, ("tensor_tensor", _c(out=th, in0=ai, in1=ai, op=ALU.mult)), reads=["rowv", "kab"], writes=["kab"])
            P.op(V, ("tensor_tensor", _c(out=den, in0=den, in1=th, op=ALU.add)), reads=["k6", "kab"], writes=["k6"])
            P.op(V, ("reciprocal", _c(out=den, in_=den)), reads=["k6"], writes=["k6"])
            P.op(V, ("tensor_tensor", _c(out=cs_, in0=nr, in1=ar, op=ALU.mult)), reads=["k5", "rowv", "k3"], writes=["k3"])
            P.op(V, ("tensor_tensor", _c(out=th, in0=sn_, in1=ai, op=ALU.mult)), reads=["k4", "rowv", "kab"], writes=["kab"])
            P.op(V, ("tensor_tensor", _c(out=cs_, in0=cs_, in1=th, op=ALU.add)), reads=["k3", "kab"], writes=["k3"])
            P.op(V, ("tensor_tensor", _c(out=cs_, in0=cs_, in1=den, op=ALU.mult)), reads=["k3", "k6"], writes=["k3"])
            P.op(V, ("tensor_tensor", _c(out=e_, in0=sn_, in1=ar, op=ALU.mult)), reads=["k4", "rowv", "k1"], writes=["k1"])
            P.op(V, ("tensor_tensor", _c(out=th, in0=nr, in1=ai, op=ALU.mult)), reads=["k5", "rowv", "kab"], writes=["kab"])
            P.op(V, ("tensor_tensor", _c(out=e_, in0=e_, in1=th, op=ALU.subtract)), reads=["k1", "kab"], writes=["k1"])
            P.op(V, ("tensor_tensor", _c(out=e_, in0=e_, in1=den, op=ALU.mult)), reads=["k1", "k6"], writes=["k1"])
            kre, kim = cs_, e_
            bre, bim = bst[:, 0, :], bst[:, 1, :]
            P.op(V, ("tensor_tensor", _c(out=sn_, in0=kre, in1=bre, op=ALU.mult)), reads=["k3", "bst", "k4"], writes=["k4"])
            P.op(V, ("tensor_tensor", _c(out=th, in0=kim, in1=bim, op=ALU.mult)), reads=["k1", "bst", "kab"], writes=["kab"])
            P.op(V, ("tensor_tensor", _c(out=sn_, in0=sn_, in1=th, op=ALU.subtract)), reads=["k4", "kab"], writes=["k4"])
            P.op(V, ("tensor_tensor", _c(out=nr, in0=kre, in1=bim, op=ALU.mult)), reads=["k3", "bst", "k5"], writes=["k5"])
            P.op(V, ("tensor_tensor", _c(out=th, in0=kim, in1=bre, op=ALU.mult)), reads=["k1", "bst", "kab"], writes=["kab"])
            P.op(V, ("tensor_tensor", _c(out=nr, in0=nr, in1=th, op=ALU.add)), reads=["k5", "kab"], writes=["k5"])
            Bre = sn_.rearrange("p (a b) -> p a b", b=64)
            Bim = nr.rearrange("p (a b) -> p a b", b=64)
            dgs = slice(pc * 8, (pc + 1) * 8)
            P.op(V, ("tensor_copy", _c(out=W1[:, dgs, 0:64], in_=Bre)), reads=["k4"], writes=["W1"])
            P.op(V, ("tensor_copy", _c(out=W1[:, dgs, 64:128], in_=Bim)), reads=["k5", "W1"], writes=["W1"])
            P.op(V, ("tensor_copy", _c(out=W2[:, dgs, 0:64], in_=Bim)), reads=["k5"], writes=["W2"])
            P.op(V, ("tensor_scalar", _c(out=W2[:, dgs, 64:128], in0=Bre, scalar1=-1.0, scalar2=None, op0=ALU.mult)), reads=["k4", "W2"], writes=["W2"])
        P.barrier()
        A32.off = mark32
        cstg = A32.take(2048)
        P.dma("sync", "cstg", cstg, s5c[l, 0], writes=["cstg"])
        P.op(V, ("tensor_scalar", _c(out=W3.rearrange("p a b -> p (a b)"), in0=cstg, scalar1=sgn, scalar2=None, op0=ALU.mult)),
             reads=["cstg", "cst"], writes=["W3"])
        P.dma("sync", "cstg", cstg, s5c[l, 1], reads=["cstg"], writes=["cstg"])
        P.op(V, ("tensor_scalar", _c(out=W4.rearrange("p a b -> p (a b)"), in0=cstg, scalar1=-1.0, scalar2=None, op0=ALU.mult)),
             reads=["cstg"], writes=["W4"])
        P.barrier()
        A32.off = mark32
        t1 = [A32.take(512) for _ in range(3)]
        t2 = [A32.take(512) for _ in range(3)]
        G = [A32.take(512) for _ in range(3)]
        tb_ = A32.take(512)
        tbi = i32c[:, 0:512]
        tmpb = A32.take(512)
        gl = A32.take(16, 2)
        A1 = [A16.take(512) for _ in range(3)]
        A2 = [A16.take(512) for _ in range(3)]
        ge_all = A16.take(1, T)
        gd = [A16.take(2, 512) for _ in range(2)]
        wg = A16.take(2, 256)
        wgs = A32.take(2, 256)
        ost = [A16.take(512) for _ in range(2)]
        gq = A32.take(512)
        gsg = A32.take(512)
        gdg = A16.take(512)
        P.dma("sync", "wgs", wgs, w_glu[l].rearrange("(kt p) n -> p kt n", p=128), writes=["wgs"])
        P.op(V, ("tensor_copy", _c(out=wg, in_=wgs)), reads=["wgs"], writes=["wg"])
        order_f = [8] + list(range(8))
        order_b = [8] + list(range(7, -1, -1))
        it = 0
        for half in range(1):
            for d in range(2):
                iota = iota_f if d == 0 else iota_r
                for gi in range(8):
                    dg = d * 8 + gi
                    rt = f"tb{gi}"
                    P.op(V, ("tensor_scalar", _c(out=tb_, in0=iota, scalar1=plw[:, 3, dg:dg + 1], scalar2=None, op0=ALU.mult)),
                         reads=["cst", "plw3"], writes=["tbb"])
                    P.op(V, ("tensor_copy", _c(out=tbi, in_=tb_)), reads=["tbb"], writes=["tbi"])
                    P.op(V, ("tensor_copy", _c(out=tmpb, in_=tbi)), reads=["tbi"], writes=["tbt"])
                    P.op(V, ("tensor_tensor", _c(out=tb_, in0=tb_, in1=tmpb, op=ALU.subtract)), reads=["tbb", "tbt"], writes=["tbb"])
                    P.op(V, ("scalar_tensor_tensor", _c(out=tb_, in0=iota, scalar=plw[:, 4, dg:dg + 1], in1=tb_, op0=ALU.mult, op1=ALU.add)),
                         reads=["cst", "plw4", "tbb"], writes=["tbb"])
                    angle_tables(None, tabS[:, gi, :], tabC[:, gi, :], tb_, 512, [("tabS", gi), ("tabC", gi)], tmpb, tbi, "tb")
                    ri = d * 8 + gi
                    P.op(V, ("tensor_scalar", _c(out=rot[:, ri, :], in0=ident, scalar1=plw[:, 7, dg:dg + 1], scalar2=None, op0=ALU.mult)),
                         reads=["cst", "plw7"], writes=[("rot", ri)])
                    P.op(V, ("scalar_tensor_tensor", _c(out=rot[:, ri, :], in0=shift64, scalar=plw[:, 9, dg:dg + 1], in1=rot[:, ri, :],
                                                                        op0=ALU.mult, op1=ALU.add)),
                         reads=["cst", "plw9", ("rot", ri)], writes=[("rot", ri)])
                    P.op(V, ("memset", _c(carry[:, gi + 8 * d, :], 0.0)), writes=[("carry", gi + 8 * d)])
                order = order_f if d == 0 else order_b
                steps = [(ci, gi) for ci in order for gi in range(8)]
                NS = len(steps)

                def geom(k):
                    ci, gi = steps[k]
                    c0, n = CHUNKS[ci]
                    tcs = slice(0, n) if (d == 0 or n == 512) else slice(256, 512)
                    return ci, gi, c0, n, tcs, d * 8 + gi, k % 3, ((0, 1) if k % 2 == 0 else (2, 3))

                def stage_a(k):
                    ci, gi, c0, n, tcs, dg, s, (b1, b2) = geom(k)
                    P.op("tensor", ("matmul", _c(pb[b1][:, 0:n], lhsT=W1[:, dg, :], rhs=uT[:, half, c0:c0 + n], start=True, stop=True)),
                         reads=["W1", "uT"], writes=[PB(b1)])
                    P.op("tensor", ("matmul", _c(pb[b2][:, 0:n], lhsT=W2[:, dg, :], rhs=uT[:, half, c0:c0 + n], start=True, stop=True)),
                         reads=["W2", "uT"], writes=[PB(b2)])
                    P.op(V, ("tensor_tensor", _c(out=t1[s][:, 0:n], in0=pb[b1][:, 0:n], in1=tabC[:, gi, tcs], op=ALU.mult)),
                         reads=[PB(b1), ("tabC", gi)], writes=[("t1", s)])
                    P.op(V, ("tensor_tensor", _c(out=t2[s][:, 0:n], in0=pb[b2][:, 0:n], in1=tabS[:, gi, tcs], op=ALU.mult)),
                         reads=[PB(b2), ("tabS", gi)], writes=[("t2", s)])
                    P.op("gpsimd", ("tensor_tensor", _c(out=t1[s][:, 0:n], in0=t1[s][:, 0:n], in1=t2[s][:, 0:n], op=ALU.add)),
                         reads=[("t1", s), ("t2", s)], writes=[("t1", s)])

                def stage_b(k):
                    ci, gi, c0, n, tcs, dg, s, _ = geom(k)
                    cres = ("carry", gi + 8 * d)
                    rcol = plw[:, 1, dg:dg + 1]
                    if d == 0:
                        g_out, g_in = G[s][:, 0:n], t1[s][:, 0:n]
                        glast = G[s][:, n - 1:n]
                    else:
                        g_out, g_in = G[s][:, 0:n][:, ::-1], t1[s][:, 0:n][:, ::-1]
                        glast = G[s][:, 0:1]
                    P.op(V, ("tensor_tensor_scan", _c(
                        out=g_out, data0=rcol.to_broadcast([128, n]), data1=g_in, initial=carry[:, gi + 8 * d, 0:1], op0=ALU.mult, op1=ALU.add)),
                        reads=[("t1", s), "plw1", cres], writes=[("G", s)])
                    P.op(V, ("tensor_copy", _c(out=gl[:, gi + 8 * d, :], in_=glast.to_broadcast([128, 2]))),
                         reads=[("G", s)], writes=[("gl", gi + 8 * d)])
                    P.op("gpsimd", ("tensor_tensor", _c(out=A1[s][:, 0:n], in0=G[s][:, 0:n], in1=tabC[:, gi, tcs], op=ALU.mult)),
                         reads=[("G", s), ("tabC", gi)], writes=[("A1", s)])
                    P.op("gpsimd", ("tensor_tensor", _c(out=A2[s][:, 0:n], in0=G[s][:, 0:n], in1=tabS[:, gi, tcs], op=ALU.mult)),
                         reads=[("G", s), ("tabS", gi)], writes=[("A2", s)])

                def stage_c(k):
                    ci, gi, c0, n, tcs, dg, s, _ = geom(k)
                    cres = ("carry", gi + 8 * d)
                    gres = ("gl", gi + 8 * d)
                    ybank = 6 + ((k // 8) % 2)
                    P.op("tensor", ("matmul", _c(pb[ybank][:, 0:n], lhsT=W3[:, dg, :], rhs=A1[s][:, 0:n], start=(gi == 0), stop=False)),
                         reads=["W3", ("A1", s)], writes=[PB(ybank)])
                    P.op("tensor", ("matmul", _c(pb[ybank][:, 0:n], lhsT=W4[:, dg, :], rhs=A2[s][:, 0:n], start=False, stop=(gi == 7))),
                         reads=["W4", ("A2", s)], writes=[PB(ybank)])
                    for kk_ in range(n // 256):
                        P.op("tensor", ("matmul", _c(pb[4][:, 2 * gi:2 * gi + 2], lhsT=rot[:, gi + 8 * d, :], rhs=gl[:, gi + 8 * d, :], start=True, stop=True)),
                             reads=[("rot", gi + 8 * d), gres], writes=[("pb4", gi)])
                        dst = gl if kk_ + 1 < n // 256 else carry
                        dres = gres if kk_ + 1 < n // 256 else cres
                        P.op("scalar", ("copy", _c(out=dst[:, gi + 8 * d, :], in_=pb[4][:, 2 * gi:2 * gi + 2])),
                             reads=[("pb4", gi)], writes=[dres])
                    if gi != 7:
                        return
                    if d == 0:
                        copy_rr(yf[:, c0:c0 + n], pb[ybank][:, 0:n], [PB(ybank)], [("yf", ci)])
                    else:
                        P.op(V, ("tensor_tensor", _c(out=yf[:, c0:c0 + n], in0=pb[ybank][:, 0:n], in1=yf[:, c0:c0 + n], op=ALU.add)),
                             reads=[PB(ybank), ("yf", ci)], writes=[("yf", ci)])
                        P.op(V, ("scalar_tensor_tensor", _c(out=yf[:, c0:c0 + n], in0=uT[:, half, c0:c0 + n], scalar=pv[:, 5:6],
                                                            in1=yf[:, c0:c0 + n], op0=ALU.mult, op1=ALU.add)),
                             reads=["uT", "pv", ("yf", ci)], writes=[("yf", ci)])
                        P.op("scalar", ("activation", _c(out=gq[:, 0:n], in_=yf[:, c0:c0 + n], func=AF.Square)), reads=[("yf", ci)], writes=["gq"])
                        P.op(V, ("tensor_scalar", _c(out=gq[:, 0:n], in0=gq[:, 0:n], scalar1=0.044715, scalar2=1.0, op0=ALU.mult, op1=ALU.add)),
                             reads=["gq"], writes=["gq"])
                        P.op(V, ("tensor_tensor", _c(out=gq[:, 0:n], in0=gq[:, 0:n], in1=yf[:, c0:c0 + n], op=ALU.mult)), reads=["gq", ("yf", ci)], writes=["gq"])
                        P.op("scalar", ("activation", _c(out=gq[:, 0:n], in_=gq[:, 0:n], func=AF.Sigmoid, scale=2.0 * math.sqrt(2.0 / math.pi))),
                             reads=["gq"], writes=["gq"])
                        P.op(V, ("tensor_tensor", _c(out=ge_all[:, half, c0:c0 + n], in0=gq[:, 0:n], in1=yf[:, c0:c0 + n], op=ALU.mult)),
                             reads=["gq", ("yf", ci)], writes=[("ge", half, ci)])

                for tck in range(NS + 2):
                    if tck < NS:
                        stage_a(tck)
                    if 0 <= tck - 1 < NS:
                        stage_b(tck - 1)
                    if 0 <= tck - 2 < NS:
                        stage_c(tck - 2)
        P.dma("gpsimd", "geL", geL, ge_all[:, 0, :], reads=[("ge", 0, ci_) for ci_ in range(9)], writes=["geL"])
        P.collective("ccg", "AllGather", ALU.bypass, PAIRS, geL, geF, reads=["geL"], writes=["geF"])
        fTv = fT.rearrange("(m p) t -> p m t", p=128)
        geFv = geF.rearrange("(kt p) t -> p kt t", p=128)
        gin_ = [A16.take(2, 512) for _ in range(2)]
        bank_rr = RR([0, 1, 2, 3])
        for ci, (c0, n) in enumerate(CHUNKS):
            if last and c0 >= NLAT:
                continue
            gdt = gd[ci % 2]
            P.dma("sync", f"gd{ci % 2}", gdt[:, 0, 0:n], fTv[:, GG + 3, c0:c0 + n], writes=[("gd", ci % 2)])
            gi_ = gin_[ci % 2]
            P.dma("sync", f"gin{ci % 2}", gi_[:, :, 0:n], geFv[:, :, c0:c0 + n], reads=["geF"], writes=[("gin", ci % 2)])
            bv = bank_rr.next()
            bg = bank_rr.next()
            for kt in range(2):
                P.op("tensor", ("matmul", _c(pb[bv][:, 0:n], lhsT=wg[:, kt, 0:128], rhs=gi_[:, kt, 0:n], start=(kt == 0), stop=(kt == 1))),
                     reads=["wg", ("gin", ci % 2)], writes=[PB(bv)])
            for kt in range(2):
                P.op("tensor", ("matmul", _c(pb[bg][:, 0:n], lhsT=wg[:, kt, 128:256], rhs=gi_[:, kt, 0:n], start=(kt == 0), stop=(kt == 1))),
                     reads=["wg", ("gin", ci % 2)], writes=[PB(bg)])
            P.op("scalar", ("activation", _c(out=gsg[:, 0:n], in_=pb[bg][:, 0:n], func=AF.Sigmoid)), reads=[PB(bg)], writes=["gsg"])
            P.op(V, ("tensor_tensor", _c(out=gsg[:, 0:n], in0=pb[bv][:, 0:n], in1=gsg[:, 0:n], op=ALU.mult)), reads=[PB(bv), "gsg"], writes=["gsg"])
            o = ost[ci % 2]
            P.op(V, ("tensor_tensor", _c(out=o[:, 0:n], in0=gsg[:, 0:n], in1=gdt[:, 0, 0:n], op=ALU.mult)),
                 reads=["gsg", ("gd", ci % 2)], writes=[("dost", ci % 2)])
            P.dma("gpsimd", f"dost{ci % 2}", mixL[384:512, c0:c0 + n], o[:, 0:n], reads=[("dost", ci % 2)],
                  writes=[("mixT", 384, c0)])

    def phase_out(l, last):
        A32.reset()
        A16.reset()
        wst = [A32.take(8, 256) for _ in range(2)]
        xk = [A32.take(512) for _ in range(3)]
        vbuf = [A32.take(8, 512) for _ in range(2)]
        st = [A32.take(512) for _ in range(4)]
        tq = [A32.take(512) for _ in range(2)]
        ostage = [A32.take(512) for _ in range(4)]
        wbf = A16.take(8, 1024)
        mx = [A16.take(8, 512) for _ in range(2)]
        vb = [A16.take(512) for _ in range(2)]
        qb = [A16.take(512) for _ in range(2)]
        wv = w_out[l].rearrange("(kt p) n -> p kt n", p=128)
        for piece in range(4):
            w = wst[piece % 2]
            cs = slice(piece * 256, (piece + 1) * 256)
            P.dma("sync", f"wst{piece % 2}", w, wv[:, :, cs], writes=[("wst", piece % 2)])
            copy_rr(wbf[:, :, cs], w, [("wst", piece % 2)], [("wbf", piece)])
        wres = [("wbf", p_) for p_ in range(4)]
        mixv = mixT.rearrange("(kt p) t -> p kt t", p=128)
        bank_rr = RR([0, 1, 2, 3])
        for ci, (c0, n) in enumerate(CHUNKS):
            if last and c0 >= NLAT:
                continue
            j = 0 if c0 < NLAT else 1
            m = mx[ci % 2]
            mres = ("mx", ci % 2)
            v = vbuf[ci % 2]
            vp = ci % 2
            P.dma("sync", f"mx{ci % 2}", m[:, :, 0:n], mixv[:, :, c0:c0 + n], writes=[mres])
            for ft in range(8):
                b = bank_rr.next()
                for kt in range(8):
                    P.op("tensor", ("matmul", _c(pb[b][:, 0:n], lhsT=wbf[:, kt, ft * 128:(ft + 1) * 128], rhs=m[:, kt, 0:n],
                                                                       start=(kt == 0), stop=(kt == 7))), reads=wres + [mres], writes=[PB(b)])
                xb = xk[ft % 3]
                xres = ("xk", ft % 3)
                P.dma("sync", f"xk{ft % 3}", xb[:, 0:n], xTv[:, ft, c0:c0 + n], reads=[("xT", ci)], writes=[xres])
                P.op("scalar", ("mul", _c(out=xb[:, 0:n], in_=xb[:, 0:n], mul=ALPHA)), reads=[xres], writes=[xres])
                vres = ("v", vp, ft)
                P.op("vector", ("scalar_tensor_tensor", _c(out=v[:, ft, 0:n], in0=pb[b][:, 0:n], scalar=modsb[:, 16 + ft, j:j + 1],
                                                                                in1=xb[:, 0:n], op0=ALU.mult, op1=ALU.add)),
                     reads=[PB(b), xres, "modsb"], writes=[vres])
                vbb = vb[ft % 2]
                qbb = qb[ft % 2]
                P.op("vector", ("tensor_copy", _c(out=vbb[:, 0:n], in_=v[:, ft, 0:n])), reads=[vres], writes=[("vb", ft % 2)])
                P.op("scalar", ("activation", _c(out=qbb[:, 0:n], in_=v[:, ft, 0:n], func=AF.Square)), reads=[vres], writes=[("qb", ft % 2)])
                P.op("tensor", ("matmul", _c(pb[4][:, 0:n], lhsT=ones_bf[:], rhs=vbb[:, 0:n], start=(ft == 0), stop=(ft == 7))),
                     reads=[("vb", ft % 2), "ones_bf"], writes=[PB(4)])
                P.op("tensor", ("matmul", _c(pb[5][:, 0:n], lhsT=ones_bf[:], rhs=qbb[:, 0:n], start=(ft == 0), stop=(ft == 7))),
                     reads=[("qb", ft % 2), "ones_bf"], writes=[PB(5)])
            mean, var, rstd, tmp = st
            P.op("vector", ("tensor_scalar", _c(out=mean[:, 0:n], in0=pb[4][:, 0:n], scalar1=1.0 / 1024.0, scalar2=None, op0=ALU.mult)),
                 reads=[PB(4)], writes=["mean"])
            P.op("vector", ("tensor_tensor", _c(out=tmp[:, 0:n], in0=mean[:, 0:n], in1=mean[:, 0:n], op=ALU.mult)), reads=["mean"], writes=["tmp"])
            P.op("vector", ("scalar_tensor_tensor", _c(out=var[:, 0:n], in0=pb[5][:, 0:n], scalar=1.0 / 1024.0, in1=tmp[:, 0:n],
                                                            op0=ALU.mult, op1=ALU.subtract)), reads=[PB(5), "tmp"], writes=["var"])
            P.op("scalar", ("activation", _c(out=rstd[:, 0:n], in_=var[:, 0:n], func=AF.Sqrt, scale=1.0, bias=LN_EPS)), reads=["var"], writes=["rstd"])
            P.op("vector", ("reciprocal", _c(out=rstd[:, 0:n], in_=rstd[:, 0:n])), reads=["rstd"], writes=["rstd"])
            for ft in range(8):
                vres = ("v", vp, ft)
                t_ = tq[ft % 2]
                tres = ("tq", ft % 2)
                P.op("vector", ("tensor_tensor", _c(out=t_[:, 0:n], in0=v[:, ft, 0:n], in1=mean[:, 0:n], op=ALU.subtract)),
                     reads=[vres, "mean"], writes=[tres])
                P.op("vector", ("tensor_tensor", _c(out=t_[:, 0:n], in0=t_[:, 0:n], in1=rstd[:, 0:n], op=ALU.mult)),
                     reads=[tres, "rstd"], writes=[tres])
                P.op("scalar", ("activation", _c(out=v[:, ft, 0:n], in_=t_[:, 0:n], func=AF.Identity,
                                                                   bias=pv[:, 15 + ft:16 + ft], scale=pv[:, 7 + ft:8 + ft])),
                     reads=[tres, "pv", vres], writes=[vres])
            if not last:
                P.dma("gpsimd", f"xst{vp}", xTv[:, :, c0:c0 + n], v[:, :, 0:n], reads=[("v", vp, f) for f in range(8)], writes=[("xT", ci)])
            else:
                for tt in range(n // 128):
                    for half in range(2):
                        b = bank_rr.next()
                        for f4 in range(4):
                            ft = half * 4 + f4
                            P.op("tensor", ("transpose", _c(out=pb[b][:, f4 * 128:(f4 + 1) * 128],
                                                                                       in_=v[:, ft, tt * 128:(tt + 1) * 128], identity=ident)),
                                 reads=[("v", vp, ft), "cst"], writes=[PB(b)])
                        og = ostage[(tt * 2 + half) % 4]
                        ogr = ("ostg", (tt * 2 + half) % 4)
                        copy_rr(og[:, 0:512], pb[b][:, 0:512], [PB(b)], [ogr])
                        P.dma("gpsimd", f"ostg{(tt * 2 + half) % 4}", out[c0 + tt * 128:c0 + (tt + 1) * 128, half * 512:(half + 1) * 512], og[:, 0:512],
                              reads=[ogr], writes=[("out", c0, tt, half)])

    def phase_D_full(l, last):
        phase_D(l, last)

    phase0()
    P.barrier()
    for l in range(n_layers):
        last = (l == DEPTH - 1)
        phase_ada(l)
        P.barrier()
        phase_in(l)
        P.barrier()
        phase_A(l, last)
        P.barrier()
        phase_B(l, last)
        P.barrier()
        phase_C(l, last)
        P.barrier()
        for kb in range(3):
            P.collective(f"ccm{kb}", "AllGather", ALU.bypass, PAIRS, mixL[kb * 128:(kb + 1) * 128, :], mixT[kb * 256:(kb + 1) * 256, :])
        phase_D_full(l, last)
        P.barrier()
        for kb in range(3, 4):
            P.collective(f"ccm{kb}", "AllGather", ALU.bypass, PAIRS, mixL[kb * 128:(kb + 1) * 128, :], mixT[kb * 256:(kb + 1) * 256, :])
        P.barrier()
        phase_out(l, last)
        P.barrier()
    P.emit()
    return nc


IN_SIZES = (256, 128, 128, 256, 256, 256, 256, 256, 256, 256, 256, 256, 256, 256)
_OFF = np.concatenate([[0], np.cumsum(IN_SIZES)]).astype(int)
(_AQ, _AK, _AV, _AG, _BQ, _BK, _BV, _BG, _CQ, _CK, _CV, _CG, _DU, _DG) = [int(v) for v in _OFF[:-1]]


def _partner_A(d):
    return d + 16 if (d % 32) < 16 else d - 16


def _partner_B(e):
    return e + 8 if (e % 16) < 8 else e - 8


def _col_index(rank):
    r = lambda a, n: list(range(a, a + n))
    rk = rank
    akv = r(_AK + rk * 64, 64) + r(_AK + (1 - rk) * 64, 64)
    main = (r(_AQ + rk * 128, 128) + akv + r(_BQ + rk * 128, 128) + r(_BK + rk * 128, 128)
            + r(_CQ + rk * 128, 128) + r(_CK + rk * 128, 128) + r(_DU + rk * 128, 128)
            + r(_AG + rk * 128, 128) + r(_BG + rk * 128, 128) + r(_CG + rk * 128, 128) + r(_DG + rk * 128, 128))
    swap = []
    for h in range(2):
        swap += [_AQ + rk * 128 + h * 64 + _partner_A(d) for d in range(64)]
    for hv in (rk, 1 - rk):
        swap += [_AK + hv * 64 + _partner_A(d) for d in range(64)]
    for base in (_BQ, _BK):
        for blk in range(4):
            swap += [base + rk * 128 + blk * 32 + _partner_B(e) for e in range(32)]
    vcols = r(_AV + rk * 64, 64) + r(_BV + rk * 128, 128) + r(_CV + rk * 128, 128)
    idx = np.array(main + swap + vcols, dtype=np.int64)
    assert idx.shape[0] == NEXT, idx.shape
    return idx


def _rope_tables():
    f32 = np.float32
    t = np.arange(NLAT)
    row = (t // GRID_W).astype(f32)
    col = (t % GRID_W).astype(f32)
    tabs = np.zeros((4, 128, T), f32)
    tabs[0] = 1.0
    tabs[2] = 1.0
    for p in range(128):
        d = p % 64
        blk, j = d // 32, d % 32
        half = 16
        inv = f32(ROPE_BASE) ** (-(f32(j % half) / f32(half)))
        ang = (row if blk == 0 else col) * f32(inv)
        tabs[0, p, :NLAT] = np.cos(ang)
        tabs[1, p, :NLAT] = (-np.sin(ang)) if (j < half) else np.sin(ang)
        e = p % 32
        blk, j = e // 16, e % 16
        half = 8
        inv = f32(ROPE_BASE) ** (-(f32(j % half) / f32(half)))
        ang = (row if blk == 0 else col) * f32(inv)
        tabs[2, p, :NLAT] = np.cos(ang)
        tabs[3, p, :NLAT] = (-np.sin(ang)) if (j < half) else np.sin(ang)
    return tabs


def _na_gather_index():
    reps = [(0, k) for k in range(4)] + [(1, k) for k in range(4)] + [(2, k) for k in range(5)] \
        + [(30, k) for k in range(28, 32)] + [(31, k) for k in range(28, 32)]
    assert len(reps) == NV
    PAD = 15 * 31
    idx = np.full((NV, 128, 128), PAD, dtype=np.int64)
    w = np.arange(64)
    cs = np.clip(w - 8, 0, 48)
    for v, (Pq, kp) in enumerate(reps):
        for rl in range(2):
            r = 2 * Pq + rl
            rs = min(max(r - 4, 0), 56)
            for krl in range(2):
                kr = 2 * kp + krl
                if not (rs <= kr < rs + 8):
                    continue
                for kc in range(64):
                    ok = (cs <= kc) & (kc < cs + 16)
                    val = (kr - r + 7) * 31 + (kc - w + 15)
                    idx[v, krl * 64 + kc, rl * 64 + w[ok]] = val[ok]
    return idx


_CONST_CACHE = {}


def _constants():
    if "c" in _CONST_CACHE:
        return _CONST_CACHE["c"]
    consts = np.zeros((128, 1288), np.float32)
    consts[:, 0:128] = np.eye(128, dtype=np.float32)
    for k in range(128):
        consts[k, 128 + (k + 64) % 128] = 1.0
    consts[:, 256:768] = np.arange(512, dtype=np.float32)[None, :]
    consts[:, 768:1280] = np.arange(511, -1, -1, dtype=np.float32)[None, :]
    consts[:64, 1280] = 1.0
    consts[64:, 1280] = -1.0
    out = dict(consts=consts, rope=_rope_tables(), naidx=_na_gather_index())
    _CONST_CACHE["c"] = out
    return out


def _mix_perm():
    perm = []
    for rk in range(2):
        for base in (0, 256, 512, 768):
            perm += list(range(base + rk * 128, base + (rk + 1) * 128))
    return np.array(perm, dtype=np.int64)


def _prep_shared(inp):
    C = _constants()
    f32 = np.float32
    sh = {}
    sh["w_ada"] = np.ascontiguousarray(inp["w_ada"], dtype=f32)
    sh["w_out"] = np.ascontiguousarray(inp["w_out"], dtype=f32)
    lam = np.concatenate([inp["lam_q1"], inp["lam_k1"], inp["lam_q2"], inp["lam_k2"]], axis=1).astype(f32)
    sh["lamv"] = np.ascontiguousarray(np.broadcast_to(lam[:, None, :], (DEPTH, 128, 128)))
    sh["rope"] = C["rope"]
    sh["consts"] = C["consts"]
    return sh


def _prep_rank(inp, rk):
    C = _constants()
    f32 = np.float32
    L = DEPTH
    sh = {}
    sh["w_ext"] = np.ascontiguousarray(inp["w_in"][:, :, _col_index(rk)], dtype=f32)
    wg = inp["w_glu"]
    sh["w_glu"] = np.ascontiguousarray(np.concatenate([wg[:, :, rk * 128:(rk + 1) * 128], wg[:, :, 256 + rk * 128:256 + (rk + 1) * 128]], axis=2), dtype=f32)
    pvec = np.zeros((L, 128, NPV), f32)
    d64 = np.arange(128) % 64
    pa = np.array([_partner_A(int(d)) for d in d64])
    pvec[:, :, 0] = inp["qn_g"][:, d64]
    pvec[:, :, 1] = inp["qn_g"][:, pa]
    pvec[:, :, 2] = inp["kn_g"][:, d64]
    pvec[:, :, 3] = inp["kn_g"][:, pa]
    pvec[:, :, 4] = inp["subln_g"][:, d64]
    pvec[:, :, 5] = inp["s5_d"].reshape(L, 2, 128)[:, rk, :]
    pvec[:, :, 7:15] = inp["ln_g"].reshape(L, 8, 128).transpose(0, 2, 1)
    pvec[:, :, 15:23] = inp["ln_b"].reshape(L, 8, 128).transpose(0, 2, 1)
    pvec[:, :, 23:47] = inp["b_ada"].reshape(L, 24, 128).transpose(0, 2, 1)
    sh["pvec"] = pvec
    rp = inp["na_rpb"].reshape(L, 4, 15 * 31).astype(f32)[:, rk * 2:rk * 2 + 2]
    rp = np.concatenate([rp, np.full((L, 2, 1), -30000.0, f32)], axis=2)
    nb = rp[:, :, C["naidx"]]
    sh["nabias"] = np.ascontiguousarray(nb.transpose(0, 1, 3, 2, 4).reshape(L, 2, 128, NV * 128))
    gs = slice(rk * 8, rk * 8 + 8)
    are = inp["s5_a_re"].astype(f32)[:, :, gs]
    aim = inp["s5_a_im"].astype(f32)[:, :, gs]
    ldt = inp["s5_log_dt"].astype(f32)[:, :, gs]
    pl = np.zeros((L, 128, 48), f32)
    a1 = are.reshape(L, 16, 64).transpose(0, 2, 1)
    a2 = aim.reshape(L, 16, 64).transpose(0, 2, 1)
    pl[:, 0:64, 0:16] = a1
    pl[:, 64:128, 0:16] = a1
    pl[:, 0:64, 16:32] = a2
    pl[:, 64:128, 16:32] = a2
    pl[:, :, 32:48] = ldt.reshape(L, 1, 16)
    sh["s5pl"] = pl
    row = np.zeros((L, 3, 1024), f32)
    row[:, 0] = are.reshape(L, 1024)
    row[:, 1] = aim.reshape(L, 1024)
    row[:, 2] = np.repeat(ldt.reshape(L, 16), 64, axis=1)
    sh["s5row"] = row
    sb = np.zeros((L, 2, 128, 1024), f32)
    sc_ = np.zeros((L, 2, 128, 2048), f32)
    bre = inp["s5_b_re"].astype(f32)[:, :, gs]
    bim = inp["s5_b_im"].astype(f32)[:, :, gs]
    cre = inp["s5_c_re"].astype(f32)[:, :, gs]
    cim = inp["s5_c_im"].astype(f32)[:, :, gs]
    for d in range(2):
        for g in range(8):
            dg = d * 8 + g
            r0 = g * 16
            sb[:, 0, r0:r0 + 16, dg * 64:(dg + 1) * 64] = bre[:, d, g].transpose(0, 2, 1)
            sb[:, 1, r0:r0 + 16, dg * 64:(dg + 1) * 64] = bim[:, d, g].transpose(0, 2, 1)
            c0 = dg * 128 + r0
            sc_[:, 0, 0:64, c0:c0 + 16] = cre[:, d, g].transpose(0, 2, 1)
            sc_[:, 0, 64:128, c0:c0 + 16] = cim[:, d, g].transpose(0, 2, 1)
            sc_[:, 1, 0:64, c0:c0 + 16] = cim[:, d, g].transpose(0, 2, 1)
            sc_[:, 1, 64:128, c0:c0 + 16] = cre[:, d, g].transpose(0, 2, 1)
    sh["s5b"] = sb
    sh["s5c"] = sc_
    return sh


def _prep_core(inp, b):
    f32 = np.float32
    m = {}
    m["x_tok"] = np.ascontiguousarray(np.concatenate([inp["x"][b], inp["ctx"][b]], axis=0), dtype=f32)
    cv = np.zeros((128, 8, 2), f32)
    cv[:, :, 0] = inp["c"][b].reshape(8, 128).T
    cv[:, :, 1] = inp["c_ctx"].reshape(8, 128).T
    m["cvec"] = cv
    return m


_PROG_CACHE = {}


def _run(inputs, n_layers=DEPTH, debug=False):
    inp = {k: np.asarray(v) for k, v in inputs.items()}
    key = (n_layers, debug)
    nc = build_program(n_layers, debug)
    sh = _prep_shared(inp)
    rk = [_prep_rank(inp, 0), _prep_rank(inp, 1)]
    in_maps = []
    for core in range(8):
        m = dict(sh)
        m.update(rk[core % 2])
        m.update(_prep_core(inp, core // 2))
        in_maps.append(m)
    res = run_bass_kernel_spmd(nc, in_maps, core_ids=list(range(8)))
    return res


def kernel(**inputs):
    res = _run(inputs)
    out = np.stack([np.asarray(res.results[2 * b]["out"], dtype=np.float32) for b in range(4)], axis=0)
    return out
```
